# Optimizing a Trainium2 kernel written in Bass

```python
import math
import jax, jax.numpy as jnp
from jax import lax
import numpy as np

D_MODEL = 1024
BATCH = 16
SEQ = 2048
DEPTH = 2
DEC_BATCH = 32
DEC_SEQ = 8
PAST_LEN = 16384
PAGE_SIZE = 128

N_HEADS = 8
HEAD_DIM = 64
N_KV = 2
HPG = N_HEADS // N_KV
D_ATT = N_HEADS * HEAD_DIM
KV_W = N_KV * HEAD_DIM
CMP_BLOCK = 32
CMP_STRIDE = 16
CMP_HIDDEN = 128
SLC_BLOCK = 64
N_SEL = 16
WINDOW = 512
Q_CHUNK = 32
S5_GROUP = 16
S5_STATE = 64
D_SSM = 512
N_SSM_GROUPS = D_SSM // S5_GROUP
N_BUCKETS = 32
MAX_DISTANCE = 128
ALPHA = (2 * DEPTH) ** 0.25
BETA = (8 * DEPTH) ** -0.25
LN_EPS = 1e-5
IN_WIDTHS = (D_ATT, KV_W, KV_W, KV_W, KV_W, KV_W, KV_W, 3 * N_HEADS, D_ATT, D_SSM, D_SSM, 2 * D_MODEL)
D_IN = sum(IN_WIDTHS)

kernel_name = 'nsa_s5_parallel_deepnorm_step'


def split_in(h):
    offs, acc = [], 0
    for w in IN_WIDTHS[:-1]:
        acc += w
        offs.append(acc)
    return jnp.split(h, offs, axis=-1)


def heads(t, n):
    return t.reshape(t.shape[:-1] + (n, HEAD_DIM))


def stack_kv(k, v):
    return jnp.stack([heads(k, N_KV), heads(v, N_KV)], axis=2)


def head_gates(g):
    return jax.nn.sigmoid(g.astype(jnp.float32)).reshape(g.shape[:-1] + (N_HEADS, 3))


def layer_norm(x, g, b):
    xf = x.astype(jnp.float32)
    mu = xf.mean(-1, keepdims=True)
    var = jnp.square(xf - mu).mean(-1, keepdims=True)
    return ((xf - mu) * lax.rsqrt(var + LN_EPS) * g + b).astype(x.dtype)


def rel_bucket(dist):
    n = jnp.maximum(dist, 0)
    max_exact = N_BUCKETS // 2
    nf = jnp.maximum(n, 1).astype(jnp.float32)
    large = max_exact + (jnp.log(nf / max_exact) / math.log(MAX_DISTANCE / max_exact)
                         * (N_BUCKETS - max_exact)).astype(jnp.int32)
    return jnp.where(n < max_exact, n, jnp.minimum(large, N_BUCKETS - 1))


def masked_softmax(logits, mask):
    l = jnp.where(mask, logits.astype(jnp.float32), -jnp.inf)
    m = jnp.max(l, axis=-1, keepdims=True)
    m = jnp.where(jnp.isfinite(m), m, 0.0)
    e = jnp.where(mask, jnp.exp(l - m), 0.0)
    s = e.sum(-1, keepdims=True)
    return e / jnp.where(s > 0, s, 1.0)


def compress(rows, w1, b1, w2, b2, pos):
    B, N = rows.shape[:2]
    nch = N // CMP_STRIDE
    ch = rows[:, :nch * CMP_STRIDE].reshape(B, nch, CMP_STRIDE, N_KV, HEAD_DIM).astype(jnp.float32)
    h_first = jnp.einsum('bcpgd,pdh->bcgh', ch, w1[:CMP_STRIDE])
    h_second = jnp.einsum('bcpgd,pdh->bcgh', ch, w1[CMP_STRIDE:])
    h = h_first[:, :-1] + h_second[:, 1:] + (jnp.einsum('pd,pdh->h', pos, w1) + b1)
    out = jnp.einsum('bcgh,hd->bcgd', jax.nn.silu(h), w2) + b2
    ends = jnp.arange(nch - 1, dtype=jnp.int32) * CMP_STRIDE + (CMP_BLOCK - 1)
    return out, ends


def to_blocks(rows):
    B, N = rows.shape[:2]
    return rows.reshape(B, N // SLC_BLOCK, SLC_BLOCK, N_KV, HEAD_DIM).transpose(0, 3, 1, 2, 4)


def nsa_attend(q, gates, q_pos, kc, vc, c_end, ks_bg, vs_bg, kw, vw, w_pos, rel_table):
    B, T = q.shape[:2]
    n_cmp = kc.shape[1]
    n_blk = ks_bg.shape[2]
    qg = (q.astype(jnp.float32) * HEAD_DIM ** -0.5).reshape(B, T, N_KV, HPG, HEAD_DIM)
    tq = q_pos[:, None]

    dist_c = tq - c_end[None, :]
    bias_c = rel_table[rel_bucket(dist_c)].reshape(T, n_cmp, N_KV, HPG).transpose(0, 2, 3, 1)
    lc = jnp.einsum('btgrd,bngd->btgrn', qg, kc) + bias_c
    pc = masked_softmax(lc, (dist_c >= 0)[:, None, None, :])
    o_c = jnp.einsum('btgrn,bngd->btgrd', pc, vc)

    ratio = SLC_BLOCK // CMP_STRIDE
    imp = jnp.pad(pc.sum(axis=3), ((0, 0), (0, 0), (0, 0), (1, ratio * n_blk - n_cmp)))
    shp = imp.shape[:-1] + (n_blk, ratio)
    imp_blk = imp[..., :ratio * n_blk].reshape(shp).sum(-1) + imp[..., 1:ratio * n_blk + 1].reshape(shp).sum(-1)
    blk = jnp.arange(n_blk, dtype=jnp.int32)[None, :]
    cur = (q_pos // SLC_BLOCK)[:, None]
    forced = (blk == 0) | (blk == cur) | (blk == cur - 1)
    visible = blk * SLC_BLOCK <= tq
    score = jnp.where(forced[:, None, :], jnp.inf, jnp.where(visible[:, None, :], imp_blk, -jnp.inf))
    _, idx = lax.top_k(score, min(N_SEL, n_blk))
    n_sel = idx.shape[-1]

    b_i = jnp.arange(B)[:, None, None, None]
    g_i = jnp.arange(N_KV)[None, None, :, None]
    k_sel = ks_bg[b_i, g_i, idx]
    v_sel = vs_bg[b_i, g_i, idx]
    s_pos = idx[..., None] * SLC_BLOCK + jnp.arange(SLC_BLOCK, dtype=jnp.int32)
    dist_s = q_pos[None, :, None, None, None] - s_pos
    bias_s = rel_table.reshape(N_BUCKETS, N_KV, HPG)[rel_bucket(dist_s), g_i[..., None]]
    ls = jnp.einsum('btgrd,btgnld->btgrnl', qg, k_sel) + jnp.moveaxis(bias_s, -1, 3)
    ps = masked_softmax(ls.reshape(B, T, N_KV, HPG, n_sel * SLC_BLOCK),
                        (dist_s >= 0).reshape(B, T, N_KV, 1, n_sel * SLC_BLOCK))
    o_s = jnp.einsum('btgrm,btgmd->btgrd', ps, v_sel.reshape(B, T, N_KV, n_sel * SLC_BLOCK, HEAD_DIM))

    dist_w = tq - w_pos[None, :]
    bias_w = rel_table[rel_bucket(dist_w)].reshape(T, -1, N_KV, HPG).transpose(0, 2, 3, 1)
    lw = jnp.einsum('btgrd,bngd->btgrn', qg, kw) + bias_w
    mw = (dist_w >= 0) & (dist_w <= WINDOW) & (w_pos[None, :] >= 0)
    pw = masked_softmax(lw, mw[:, None, None, :])
    o_w = jnp.einsum('btgrn,bngd->btgrd', pw, vw)

    g = gates.reshape(B, T, N_KV, HPG, 3)
    o = g[..., 0:1] * o_c + g[..., 1:2] * o_s + g[..., 2:3] * o_w
    return o.reshape(B, T, D_ATT)


def nsa_prompt(q, gates, kv_c, kv_s, kv_w, cmp_k, cmp_v, rel_table):
    B, S = q.shape[:2]
    kc, c_end = compress(kv_c[:, :, 0], *cmp_k)
    vc, _ = compress(kv_c[:, :, 1], *cmp_v)
    ks_bg, vs_bg = to_blocks(kv_s[:, :, 0]), to_blocks(kv_s[:, :, 1])
    pad = ((0, 0), (WINDOW, 0), (0, 0), (0, 0))
    kw_pad, vw_pad = jnp.pad(kv_w[:, :, 0], pad), jnp.pad(kv_w[:, :, 1], pad)

    def one_block(c0):
        qc = lax.dynamic_slice_in_dim(q, c0, Q_CHUNK, axis=1)
        gc = lax.dynamic_slice_in_dim(gates, c0, Q_CHUNK, axis=1)
        kw = lax.dynamic_slice_in_dim(kw_pad, c0, WINDOW + Q_CHUNK, axis=1)
        vw = lax.dynamic_slice_in_dim(vw_pad, c0, WINDOW + Q_CHUNK, axis=1)
        q_pos = c0 + jnp.arange(Q_CHUNK, dtype=jnp.int32)
        w_pos = c0 - WINDOW + jnp.arange(WINDOW + Q_CHUNK, dtype=jnp.int32)
        return nsa_attend(qc, gc, q_pos, kc, vc, c_end, ks_bg, vs_bg, kw, vw, w_pos, rel_table)

    out = lax.map(one_block, jnp.arange(0, S, Q_CHUNK, dtype=jnp.int32))
    return out.transpose(1, 0, 2, 3).reshape(B, S, D_ATT)


def nsa_sample(q, gates, all_c, all_s, all_w, past_len, cmp_k, cmp_v, rel_table):
    T = q.shape[1]
    N = all_c.shape[1]
    kc, c_end = compress(all_c[:, :, 0], *cmp_k)
    vc, _ = compress(all_c[:, :, 1], *cmp_v)
    ns = -(-N // SLC_BLOCK)
    slc = jnp.pad(all_s, ((0, 0), (0, ns * SLC_BLOCK - N), (0, 0), (0, 0), (0, 0)))
    ks_bg, vs_bg = to_blocks(slc[:, :, 0]), to_blocks(slc[:, :, 1])
    n_w = all_w.shape[1]
    w_pos = (N - n_w) + jnp.arange(n_w, dtype=jnp.int32)
    q_pos = past_len + jnp.arange(T, dtype=jnp.int32)
    return nsa_attend(q, gates, q_pos, kc, vc, c_end, ks_bg, vs_bg, all_w[:, :, 0], all_w[:, :, 1], w_pos, rel_table)


def gather_pages(pool, page_table):
    g = pool[page_table]
    return g.reshape((g.shape[0], g.shape[1] * g.shape[2]) + g.shape[3:])


def s5_combine(e1, e2):
    a1r, a1i, b1r, b1i = e1
    a2r, a2i, b2r, b2i = e2
    return (a2r * a1r - a2i * a1i, a2r * a1i + a2i * a1r,
            a2r * b1r - a2i * b1i + b2r, a2r * b1i + a2i * b1r + b2i)


def s5_scan(u, h0_re, h0_im, a_re, a_im, log_dt, b_re, b_im, c_re, c_im, d):
    f32 = jnp.float32
    u = u.astype(f32)
    a_re, a_im = a_re.astype(f32), a_im.astype(f32)
    dt = jnp.exp(log_dt.astype(f32))[:, None]
    mag = jnp.exp(dt * a_re)
    ab_re, ab_im = mag * jnp.cos(dt * a_im), mag * jnp.sin(dt * a_im)
    den = a_re * a_re + a_im * a_im
    x_re, x_im = ab_re - 1.0, ab_im
    k_re = (x_re * a_re + x_im * a_im) / den
    k_im = (x_im * a_re - x_re * a_im) / den
    bb_re = k_re[..., None] * b_re - k_im[..., None] * b_im
    bb_im = k_re[..., None] * b_im + k_im[..., None] * b_re
    bu_re = jnp.einsum('btgc,gpc->btgp', u, bb_re)
    bu_im = jnp.einsum('btgc,gpc->btgp', u, bb_im)
    h0_re, h0_im = h0_re.astype(f32), h0_im.astype(f32)
    bu_re = bu_re.at[:, 0].add(ab_re * h0_re - ab_im * h0_im)
    bu_im = bu_im.at[:, 0].add(ab_re * h0_im + ab_im * h0_re)
    T = u.shape[1]
    a_seq_re = jnp.broadcast_to(ab_re, (1, T) + ab_re.shape)
    a_seq_im = jnp.broadcast_to(ab_im, (1, T) + ab_im.shape)
    _, _, h_re, h_im = lax.associative_scan(s5_combine, (a_seq_re, a_seq_im, bu_re, bu_im), axis=1)
    y = (jnp.einsum('btgp,gcp->btgc', h_re, c_re) - jnp.einsum('btgp,gcp->btgc', h_im, c_im) + d * u)
    return y, h_re[:, -1], h_im[:, -1]


def s5_branch(u, h0_re, h0_im, ssm_par, glu_w, glu_b):
    B, T = u.shape[:2]
    y, hr, hi = s5_scan(u.reshape(B, T, N_SSM_GROUPS, S5_GROUP), h0_re, h0_im, *ssm_par)
    y = jax.nn.gelu(y.reshape(B, T, D_SSM))
    out = (y @ glu_w[0] + glu_b[0]) * jax.nn.sigmoid(y @ glu_w[1] + glu_b[1])
    return out, hr, hi


def merge_out(x, o_att, z_att, y_ssm, z_ssm, g_merge, w_att_out, w_ssm_out, w_o, ln_g, ln_b):
    p_att = (o_att * jax.nn.silu(z_att)) @ w_att_out
    p_ssm = (y_ssm * jax.nn.silu(z_ssm)) @ w_ssm_out
    g_att, g_ssm = jnp.split(g_merge, 2, axis=-1)
    merged = jax.nn.sigmoid(g_att) * p_att + jax.nn.sigmoid(g_ssm) * p_ssm
    return layer_norm(ALPHA * x + merged @ w_o, ln_g, ln_b)


def setup_inputs(seed: int = 0) -> dict:
    key = jax.random.key(seed)
    ks = jax.random.split(key, 32)
    f32 = jnp.float32

    def nrm(k, shape, scale):
        return jax.random.normal(k, shape, f32) * scale

    n_pages = PAST_LEN // PAGE_SIZE
    n_used = DEC_BATCH * n_pages
    n_pool = n_used + max(1, n_used // 4)
    w_buf = min(WINDOW, PAST_LEN)
    kv_page = (DEPTH, n_pool, PAGE_SIZE, 2, N_KV, HEAD_DIM)
    page_table = jax.random.permutation(ks[5], n_pool)[:n_used].reshape(DEC_BATCH, n_pages).astype(jnp.int32)
    ssm_st = (DEPTH, DEC_BATCH, N_SSM_GROUPS, S5_STATE)
    return {
        'x_prompt': nrm(ks[0], (BATCH, SEQ, D_MODEL), 1.0),
        'x_sample': nrm(ks[1], (DEC_BATCH, DEC_SEQ, D_MODEL), 1.0),
        'cache_kv_cmp': nrm(ks[2], kv_page, 1.0),
        'cache_kv_slc': nrm(ks[3], kv_page, 1.0),
        'state_win_kv': nrm(ks[4], (DEPTH, DEC_BATCH, w_buf, 2, N_KV, HEAD_DIM), 1.0),
        'state_ssm_re': nrm(ks[6], ssm_st, 0.1),
        'state_ssm_im': nrm(ks[7], ssm_st, 0.1),
        'page_table': page_table,
        'rel_bias': nrm(ks[8], (N_BUCKETS, N_HEADS), 0.5),
        'w_in': nrm(ks[9], (DEPTH, D_MODEL, D_IN), D_MODEL ** -0.5),
        'cmp_w1': nrm(ks[10], (DEPTH, 2, CMP_BLOCK, HEAD_DIM, CMP_HIDDEN), (CMP_BLOCK * HEAD_DIM) ** -0.5),
        'cmp_b1': nrm(ks[11], (DEPTH, 2, CMP_HIDDEN), 0.01),
        'cmp_w2': nrm(ks[12], (DEPTH, 2, CMP_HIDDEN, HEAD_DIM), CMP_HIDDEN ** -0.5),
        'cmp_b2': nrm(ks[13], (DEPTH, 2, HEAD_DIM), 0.01),
        'cmp_pos': nrm(ks[14], (DEPTH, 2, CMP_BLOCK, HEAD_DIM), 0.1),
        'ssm_a_re': -0.5 * jnp.exp(nrm(ks[15], (DEPTH, N_SSM_GROUPS, S5_STATE), 0.05)),
        'ssm_a_im': jnp.broadcast_to(math.pi * jnp.arange(S5_STATE, dtype=f32), (DEPTH, N_SSM_GROUPS, S5_STATE)),
        'ssm_log_dt': jax.random.uniform(ks[16], (DEPTH, N_SSM_GROUPS), f32, math.log(1e-3), math.log(1e-1)),
        'ssm_b_re': nrm(ks[17], (DEPTH, N_SSM_GROUPS, S5_STATE, S5_GROUP), (2 * S5_GROUP) ** -0.5),
        'ssm_b_im': nrm(ks[18], (DEPTH, N_SSM_GROUPS, S5_STATE, S5_GROUP), (2 * S5_GROUP) ** -0.5),
        'ssm_c_re': nrm(ks[19], (DEPTH, N_SSM_GROUPS, S5_GROUP, S5_STATE), (2 * S5_STATE) ** -0.5),
        'ssm_c_im': nrm(ks[20], (DEPTH, N_SSM_GROUPS, S5_GROUP, S5_STATE), (2 * S5_STATE) ** -0.5),
        'ssm_d': nrm(ks[21], (DEPTH, N_SSM_GROUPS, S5_GROUP), 1.0),
        'ssm_glu_w': nrm(ks[22], (DEPTH, 2, D_SSM, D_SSM), D_SSM ** -0.5),
        'ssm_glu_b': nrm(ks[23], (DEPTH, 2, D_SSM), 0.01),
        'w_att_out': nrm(ks[24], (DEPTH, D_ATT, D_MODEL), BETA * D_ATT ** -0.5),
        'w_ssm_out': nrm(ks[25], (DEPTH, D_SSM, D_MODEL), BETA * D_SSM ** -0.5),
        'w_o': nrm(ks[26], (DEPTH, D_MODEL, D_MODEL), BETA * D_MODEL ** -0.5),
        'ln_g': 1.0 + nrm(ks[27], (DEPTH, D_MODEL), 0.01),
        'ln_b': nrm(ks[28], (DEPTH, D_MODEL), 0.01),
    }


def reference(x_prompt, x_sample, cache_kv_cmp, cache_kv_slc, state_win_kv, state_ssm_re, state_ssm_im,
              page_table, rel_bias, w_in, cmp_w1, cmp_b1, cmp_w2, cmp_b2, cmp_pos,
              ssm_a_re, ssm_a_im, ssm_log_dt, ssm_b_re, ssm_b_im, ssm_c_re, ssm_c_im, ssm_d,
              ssm_glu_w, ssm_glu_b, w_att_out, w_ssm_out, w_o, ln_g, ln_b):
    past_len = page_table.shape[1] * cache_kv_cmp.shape[2]
    bp, seq = x_prompt.shape[:2]
    w_buf = state_win_kv.shape[2]
    h_p, h_s = x_prompt, x_sample
    p_cmp, p_slc, p_win, p_re, p_im = [], [], [], [], []
    s_cmp, s_slc, s_win, s_re, s_im = [], [], [], [], []
    for l in range(DEPTH):
        cmp_k = (cmp_w1[l, 0], cmp_b1[l, 0], cmp_w2[l, 0], cmp_b2[l, 0], cmp_pos[l, 0])
        cmp_v = (cmp_w1[l, 1], cmp_b1[l, 1], cmp_w2[l, 1], cmp_b2[l, 1], cmp_pos[l, 1])
        ssm_l = (ssm_a_re[l], ssm_a_im[l], ssm_log_dt[l], ssm_b_re[l], ssm_b_im[l],
                 ssm_c_re[l], ssm_c_im[l], ssm_d[l])
        out_l = (w_att_out[l], w_ssm_out[l], w_o[l], ln_g[l], ln_b[l])

        q, kc, vc, ksl, vsl, kwn, vwn, gn, za, u, zs, gm = split_in(h_p @ w_in[l])
        kv_c, kv_s, kv_w = stack_kv(kc, vc), stack_kv(ksl, vsl), stack_kv(kwn, vwn)
        o_att = nsa_prompt(heads(q, N_HEADS), head_gates(gn), kv_c, kv_s, kv_w, cmp_k, cmp_v, rel_bias)
        h0 = jnp.zeros((bp, N_SSM_GROUPS, S5_STATE), jnp.float32)
        y_ssm, hr, hi = s5_branch(u, h0, h0, ssm_l, ssm_glu_w[l], ssm_glu_b[l])
        p_cmp.append(kv_c)
        p_slc.append(kv_s)
        p_win.append(kv_w[:, seq - min(WINDOW, seq):])
        p_re.append(hr)
        p_im.append(hi)
        h_p = merge_out(h_p, o_att, za, y_ssm, zs, gm, *out_l)

        q, kc, vc, ksl, vsl, kwn, vwn, gn, za, u, zs, gm = split_in(h_s @ w_in[l])
        kv_c, kv_s, kv_w = stack_kv(kc, vc), stack_kv(ksl, vsl), stack_kv(kwn, vwn)
        all_c = jnp.concatenate([gather_pages(cache_kv_cmp[l], page_table), kv_c.astype(cache_kv_cmp.dtype)], axis=1)
        all_s = jnp.concatenate([gather_pages(cache_kv_slc[l], page_table), kv_s.astype(cache_kv_slc.dtype)], axis=1)
        all_w = jnp.concatenate([state_win_kv[l], kv_w.astype(state_win_kv.dtype)], axis=1)
        o_att = nsa_sample(heads(q, N_HEADS), head_gates(gn), all_c, all_s, all_w, past_len, cmp_k, cmp_v, rel_bias)
        y_ssm, hr, hi = s5_branch(u, state_ssm_re[l], state_ssm_im[l], ssm_l, ssm_glu_w[l], ssm_glu_b[l])
        s_cmp.append(kv_c)
        s_slc.append(kv_s)
        s_win.append(all_w[:, all_w.shape[1] - w_buf:])
        s_re.append(hr)
        s_im.append(hi)
        h_s = merge_out(h_s, o_att, za, y_ssm, zs, gm, *out_l)

    return (h_p, h_s,
            jnp.stack(p_cmp), jnp.stack(p_slc), jnp.stack(p_win), jnp.stack(p_re), jnp.stack(p_im),
            jnp.stack(s_cmp), jnp.stack(s_slc), jnp.stack(s_win), jnp.stack(s_re), jnp.stack(s_im))
```

```python
import numpy as np
from contextlib import ExitStack
import concourse.bass as bass
import concourse.mybir as mybir
from concourse.bass_utils import run_bass_kernel_spmd

F32 = mybir.dt.float32
BF16 = mybir.dt.bfloat16
I32 = mybir.dt.int32
AF = mybir.ActivationFunctionType
ALU = mybir.AluOpType

D_MODEL = 1024
DEPTH = 2
SEQ = 2048
BATCH = 16
DEC_BATCH = 32
DEC_SEQ = 8
D_IN = 4888
WINDOW = 512
ALPHA = (2 * DEPTH) ** 0.25
LN_EPS = 1e-5
N_CORES = 8
DBG_NOREAD = False
DBG_OUT = False
DBG_STAGE = 9
DBG_BR = 'csw'
TWO_PI = 6.283185307179586
C_Q, C_KV, C_GN, C_ZA, C_U, C_ZS, C_GM = 0, 512, 1280, 1304, 1816, 2328, 2840


class Sched:
    EPOCH = 4000
    NSLOT = 16

    def __init__(self, nc, es):
        self.nc = nc
        self.es = es
        self.eng = {'pe': nc.tensor, 'dve': nc.vector, 'act': nc.scalar, 'pool': nc.gpsimd, 'sp': nc.sync}
        self.cnt = {e: 0 for e in self.eng}
        self.sems = {}
        self.waited = {e: {} for e in self.eng}
        self.last_w = {}
        self.readers = {}
        self.slot_sem = [es.enter_context(nc.semaphore(f"dslot{i}")) for i in range(self.NSLOT)]
        self.slot_cnt = [0] * self.NSLOT
        self.slot_next = 0
        self.idma_sems = {i: es.enter_context(nc.semaphore(f"idma{i}")) for i in range(28)}
        self.idma_used = {i: 0 for i in range(28)}
        self.idma_next = 0
        self.bound_regs = {}
        self.cc_sems = []
        self.cc_pool = [es.enter_context(nc.semaphore(f"cc{i}")) for i in range(16)]

    def _sem(self, e, epoch):
        k = (e, epoch)
        if k not in self.sems:
            self.sems[k] = self.es.enter_context(self.nc.semaphore(f"s_{e}_{epoch}"))
        return self.sems[k]

    def _wait(self, e, dep):
        kind, src, n = dep
        key = (kind, src)
        if self.waited[e].get(key, 0) >= n:
            return
        self.waited[e][key] = n
        if kind == 'eng':
            epoch, v = divmod(n - 1, self.EPOCH)
            self.eng[e].wait_ge(self._sem(src, epoch), v + 1)
        elif kind == 'cc':
            self.eng[e].wait_ge(self.cc_sems[src], n)
        elif kind == 'idma':
            self.eng[e].wait_ge(self.idma_sems[src], 16 * n)
        else:
            self.eng[e].wait_ge(self.slot_sem[src], 16 * n)

    def _deps(self, e, reads, writes, pe_accum=False):
        deps = []
        for b in reads:
            if b in self.last_w:
                deps.append(self.last_w[b])
        for b in writes:
            if b in self.last_w:
                deps.append(self.last_w[b])
            deps.extend(self.readers.get(b, {}).values())
        for d in deps:
            if pe_accum and d[0] == 'eng' and d[1] == 'pe' and e == 'pe':
                continue
            self._wait(e, d)

    def _record(self, tag, reads, writes):
        for b in reads:
            self.readers.setdefault(b, {})[tag[:2]] = tag
        for b in writes:
            self.last_w[b] = tag
            self.readers[b] = {}

    def op(self, e, fn, reads=(), writes=(), pe_accum=False):
        self._deps(e, reads, writes, pe_accum)
        ins = fn(self.eng[e])
        self.cnt[e] += 1
        n = self.cnt[e]
        epoch, v = divmod(n - 1, self.EPOCH)
        ins.then_inc(self._sem(e, epoch), 1)
        self._record(('eng', e, n), reads, writes)
        return ins

    def dma(self, out, in_, reads=(), writes=(), q='sp', **kw):
        s = self.slot_next
        self.slot_next = (s + 1) % self.NSLOT
        if self.slot_cnt[s] > 0:
            self._wait(q, ('dma', s, self.slot_cnt[s]))
        self._deps(q, reads, writes)
        ins = self.eng[q].dma_start(out=out, in_=in_, **kw)
        ins.then_inc(self.slot_sem[s], 16)
        self.slot_cnt[s] += 1
        self._record(('dma', s, self.slot_cnt[s]), reads, writes)
        return ins

    def collective(self, fn, reads=(), writes=()):
        self._deps('pool', reads, writes)
        sem = self.cc_pool[len(self.cc_sems)]
        self.cc_sems.append(sem)
        ins = fn(self.eng['pool'])
        ins.then_inc(sem, 1)
        self._record(('cc', len(self.cc_sems) - 1, 1), reads, writes)
        return ins

    def idma(self, out, srcs, reads=(), writes=()):
        q = 'pool'
        s_ = self.idma_next
        self.idma_next = (s_ + 1) % len(self.idma_sems)
        if self.idma_used[s_] > 0:
            self._wait(q, ('idma', s_, self.idma_used[s_]))
        self._deps(q, reads, writes)
        for (src, idx_ap, bound) in srcs:
            if bound is None:
                kw = {}
            else:
                if bound not in self.bound_regs:
                    self.bound_regs[bound] = self.eng[q].to_reg(bound)
                kw = dict(bounds_check=self.bound_regs[bound], oob_is_err=False)
            ins = self.eng[q].indirect_dma_start(out=out, out_offset=None, in_=src,
                                                 in_offset=bass.IndirectOffsetOnAxis(ap=idx_ap, axis=0), **kw)
            ins.then_inc(self.idma_sems[s_], 16)
            self.idma_used[s_] += 1
        self._record(('idma', s_, self.idma_used[s_]), reads, writes)

    def barrier(self):
        for e in self.eng:
            for s in range(self.NSLOT):
                if self.slot_cnt[s] > 0:
                    self._wait(e, ('dma', s, self.slot_cnt[s]))
            for e2 in self.eng:
                if e2 != 'sp' and e2 != e and self.cnt[e2] > 0:
                    self._wait(e, ('eng', e2, self.cnt[e2]))
        for e in self.eng:
            for slot, n in getattr(self, 'idma_used', {}).items():
                if n > 0:
                    self._wait(e, ('idma', slot, n))
        keep_w = {k: v for k, v in self.last_w.items() if v[0] == 'cc'}
        self.last_w = keep_w
        self.readers = {}

    def finish(self):
        for s in range(self.NSLOT):
            if self.slot_cnt[s] > 0:
                self._wait('sp', ('dma', s, self.slot_cnt[s]))
        for e in self.eng:
            if e != 'sp' and self.cnt[e] > 0:
                self._wait('sp', ('eng', e, self.cnt[e]))


def build(NS=2, T=SEQ, depth=DEPTH, NSB=4, ncores=N_CORES):
    nc = bass.Bass("TRN2", target_bir_lowering=False)
    NT = T // 128
    NCH = T // 512
    WIN = min(WINDOW, T)

    def din(name, shape, dt=F32):
        return nc.dram_tensor(name, list(shape), dt, kind="ExternalInput").ap()

    def dout(name, shape, dt=F32):
        return nc.dram_tensor(name, list(shape), dt, kind="ExternalOutput").ap()

    x_in = din("x", [NS, T, D_MODEL])
    w_in = din("w_in", [depth, D_MODEL, D_IN])
    w_att_out = din("w_att_out", [depth, 512, D_MODEL])
    w_ssm_out = din("w_ssm_out", [depth, 512, D_MODEL])
    w_o = din("w_o", [depth, D_MODEL, D_MODEL])
    ln_g = din("ln_g", [depth, D_MODEL])
    ln_b = din("ln_b", [depth, D_MODEL])
    ident_in = din("ident", [128, 128])
    s5_are_in = din("s5_are", [depth, 128, 16])
    s5_aim_in = din("s5_aim", [depth, 128, 16])
    s5_ldt_in = din("s5_ldt", [depth, 128, 16])
    s5_bre_in = din("s5_bre", [depth, 128, 256])
    s5_bim_in = din("s5_bim", [depth, 128, 256])
    s5_cre_in = din("s5_cre", [depth, 128, 256])
    s5_cim_in = din("s5_cim", [depth, 128, 256])
    s5_d_in = din("s5_d", [depth, 128, 4])
    glu_w_in = din("glu_w", [depth, 2, 512, 512])
    glu_bc_in = din("glu_bc", [depth, 128, 8])
    bdmask_in = din("bdmask", [128, 128])
    NCMP = T // 16 - 1
    NBLK = T // 64
    NV = 4608
    OFFV = 2048
    rel_bias_in = din("rel_bias", [32, 8])
    oh_in = din("oh_bias", [33, NV])
    wcut_in = din("wcut", [1, NV])
    emat_in = din("emat", [32, NT, 128])
    fv_in = din("fv_add", [128, NT, 32])
    vis_in = din("vis_mul", [128, NT, 32])
    mimp_in = din("mimp", [128, 32])
    cmp_w1_in = din("cmp_w1", [depth, 2, 2048, 128])
    cmp_w2_in = din("cmp_w2", [depth, 2, 128, 64])
    cmp_posr_in = din("cmp_posr", [depth, 2, 128, 16])
    cmp_b1c_in = din("cmp_b1c", [depth, 128, 2])
    cmp_b2c_in = din("cmp_b2c", [depth, 128, 1])
    cmp_b2r_in = din("cmp_b2r", [depth, 1, 64])
    NQ = NSB * 8
    PAGES = 128
    ROWS_SH = 5120 // N_CORES * 128 if ncores > 1 else None
    if NSB:
        xs_in = din("xs", [NQ, D_MODEL])
        pool_rows = 5120 * 128 if ncores > 1 else NSB * 160 * 128
        NCHK = 2 if ncores > 1 else 1
        ch_rows = pool_rows // NCHK
        sh_rows = ch_rows // ncores
        cmp_sh = din("cmp_sh", [depth, NCHK, sh_rows, 256])
        slc_sh = din("slc_sh", [depth, NCHK, sh_rows, 256])
        swin_in = din("swin", [depth, NSB, 512, 256])
        h0re_in = din("h0re", [depth, 128, 16 * NSB])
        h0im_in = din("h0im", [depth, 128, 16 * NSB])
        pt_in = din("pt", [1, NSB * PAGES], I32)
        d8_in = din("d8", [8, 32])
        fvs_in = din("fv_s", [8, 257])
        viss_in = din("vis_s", [8, 257])
        m33_in = din("m33", [128, 33])
        ys_out = dout("ys", [NQ, D_MODEL])
        skvc_out = dout("skvc", [depth, NSB, 8, 256])
        skvs_out = dout("skvs", [depth, NSB, 8, 256])
        swin_out = dout("swin_o", [depth, NSB, 512, 256])
        sssm_re_out = dout("sssm_re", [depth, NSB, 32, 64])
        sssm_im_out = dout("sssm_im", [depth, NSB, 32, 64])
        dbg_os = dout("dbg_os", [NQ, 512]) if DBG_OUT else None
        dbg_madd = dout("dbg_madd", [8, 520], BF16) if DBG_OUT else None
        dbg_imp = dout("dbg_imp", [8, 514]) if DBG_OUT else None
        cin = [[nc.dram_tensor(f"cin{l}_{k}", [sh_rows, 256], BF16, kind="Internal").ap() for k in range(NCHK)] for l in range(depth)]
        sin_ = [[nc.dram_tensor(f"sin{l}_{k}", [sh_rows, 256], BF16, kind="Internal").ap() for k in range(NCHK)] for l in range(depth)]
        if ncores > 1:
            cfull = [[nc.dram_tensor(f"cfull{l}_{k}", [ch_rows, 256], BF16, kind="Internal").ap() for k in range(NCHK)] for l in range(depth)]
            sfull = [[nc.dram_tensor(f"sfull{l}_{k}", [ch_rows, 256], BF16, kind="Internal").ap() for k in range(NCHK)] for l in range(depth)]
        else:
            cfull, sfull = cin, sin_
    rep = nc.dram_tensor("rep_bias", [9, 128, NV], BF16, kind="Internal").ap()
    scr_v = nc.dram_tensor("scr_v", [9, NV], BF16, kind="Internal").ap()

    y_out = dout("y", [NS, T, D_MODEL])
    kvc_out = dout("kvc", [depth, NS, T, 256])
    kvs_out = dout("kvs", [depth, NS, T, 256])
    kvw_out = dout("kvw", [depth, NS, WIN, 256])
    ssm_re_out = dout("ssm_re", [depth, NS, 32, 64])
    ssm_im_out = dout("ssm_im", [depth, NS, 32, 64])
    dbg_s = dout("dbg_s", [4, 128, T], BF16) if DBG_OUT else None
    dbg_o = dout("dbg_o", [T, 512], BF16) if DBG_OUT else None
    xmid = y_out

    with ExitStack() as es:
        S = Sched(nc, es)

        uid = [0]

        def sb(name, shape, dt=F32, st=es):
            uid[0] += 1
            return st.enter_context(nc.sbuf_tensor(f"{name}_{uid[0]}", list(shape), dt))

        ps = [es.enter_context(nc.psum_tensor(f"ps{i}", [128, 512], F32)) for i in range(6)]
        ps_rr = [0]

        def next_ps(lo=0, hi=4):
            i = lo + ps_rr[0] % (hi - lo)
            ps_rr[0] += 1
            return ps[i], f"ps{i}"

        ident_f = sb("ident_f", [128, 128], F32)
        ident_b = sb("ident_b", [128, 128], BF16)
        S.dma(ident_f[:], ident_in, writes=['ident_f'])
        S.op('dve', lambda e: e.tensor_copy(ident_b[:], ident_f[:]), reads=['ident_f'], writes=['ident_b'])
        lng = sb("lng", [128, D_MODEL], F32)
        lnb = sb("lnb", [128, D_MODEL], F32)
        epsc = sb("epsc", [128, 1], F32)
        S.op('dve', lambda e: e.memset(epsc[:], LN_EPS), writes=['epsc'])

        bdmask = sb("bdmask", [128, 128], F32)
        S.dma(bdmask[:], bdmask_in, writes=['bdmask'])

        def bcast(t_ap, pattern):
            return bass.AP(t_ap.tensor, t_ap.offset, [list(t_ap.ap[0])] + [list(p) for p in pattern])

        b31col = sb("b31col", [128, 8], F32)
        S.dma(b31col[:], bass.AP(rel_bias_in.tensor, 31 * 8, [[0, 128], [1, 8]]), writes=['b31col'])
        tbwc = sb("tbwc", [128, 1408], BF16)
        emat = sb("emat", [32, NT, 128], BF16)
        fv_add = sb("fv_add", [128, NT, 32], F32)
        vis_mul = sb("vis_mul", [128, NT, 32], F32)
        mimp_b = sb("mimp_b", [128, 32], BF16)
        ones_row = sb("ones_row", [1, 128], BF16)
        S.op('dve', lambda e: e.memset(ones_row[:], 1.0), writes=['ones_row'])
        S.dma(fv_add[:], fv_in, writes=['fv_add'])
        S.dma(vis_mul[:], vis_in, writes=['vis_mul'])
        with ExitStack() as tst:
            rbx = sb("rbx", [33, 8], F32, tst)
            ohs = sb("ohs", [33, NV], F32, tst)
            vbs = sb("vbs", [8, NV], BF16, tst)
            wcf = sb("wcf", [1, NV], F32, tst)
            wcb = sb("wcb", [1, NV], BF16, tst)
            repb = sb("repb", [128, NV], BF16, tst)
            emf = sb("emf", [32, NT, 128], F32, tst)
            mif = sb("mif", [128, 32], F32, tst)
            S.dma(emf[:], emat_in, writes=['emf'])
            S.op('dve', lambda e: e.tensor_copy(emat[:], emf[:]), reads=['emf'], writes=['emat'])
            S.dma(mif[:], mimp_in, writes=['mif'])
            S.op('dve', lambda e: e.tensor_copy(mimp_b[:], mif[:]), reads=['mif'], writes=['mimp_b'])
            S.op('dve', lambda e: e.memset(rbx[32:33, :], -30000.0), writes=['rbx32'])
            S.dma(rbx[0:32, :], rel_bias_in, writes=['rbx'])
            S.dma(ohs[:], oh_in, writes=['ohs'])
            S.dma(wcf[:], wcut_in, writes=['wcf'])
            S.op('dve', lambda e: e.tensor_copy(wcb[:], wcf[:]), reads=['wcf'], writes=['wcb'])
            S.dma(scr_v[8:9, :], wcb[:], reads=['wcb'], writes=['scr_v'])
            for cc in range(NV // 512):
                p, pn = next_ps()
                S.op('pe', lambda e, cc=cc: e.matmul(p[0:8, :], rbx[:, :], ohs[:, cc * 512:(cc + 1) * 512], start=True, stop=True),
                     reads=['rbx', 'rbx32', 'ohs'], writes=[pn])
                S.op('act', lambda e, cc=cc: e.activation(vbs[:, cc * 512:(cc + 1) * 512], p[0:8, :], AF.Copy), reads=[pn], writes=['vbs'])
            S.dma(scr_v[0:8, :], vbs[:], reads=['vbs'], writes=['scr_v'])
            for hh in range(9):
                S.dma(repb[:], bass.AP(scr_v.tensor, hh * NV, [[0, 128], [1, NV]]), reads=['scr_v'], writes=['repb'])
                S.dma(rep[hh], repb[:], reads=['repb'], writes=['rep'])
            S.dma(tbwc[:], bass.AP(rep.tensor, 8 * 128 * NV + OFFV - 384, [[NV - 1, 128], [1, 1408]]), reads=['rep'], writes=['tbwc'])
            S.barrier()

        if NSB:
            with ExitStack() as tst:
                RB = 8
                cvf = [sb(f"cvf{i}", [128, RB * 256], F32, tst) for i in range(2)]
                cvb = [sb(f"cvb{i}", [128, RB * 256], BF16, tst) for i in range(2)]
                rpp = sh_rows // 128
                npc = 0
                for l in range(depth):
                    for (sh, ci, cf, nm) in ((cmp_sh, cin, cfull, 'c'), (slc_sh, sin_, sfull, 's')):
                        for k in range(NCHK):
                            src3 = sh[l, k].rearrange("(p i) c -> p (i c)", p=128)
                            dst3 = ci[l][k].rearrange("(p i) c -> p (i c)", p=128)
                            for r0_ in range(0, rpp, RB):
                                nr = min(RB, rpp - r0_)
                                i = npc % 2
                                npc += 1
                                S.dma(cvf[i][:, 0:nr * 256], src3[:, r0_ * 256:(r0_ + nr) * 256], writes=[f'cvf{i}'])
                                eng = ('act', 'dve', 'pool')[npc % 3]
                                if eng == 'act':
                                    S.op('act', lambda e: e.activation(cvb[i][:, 0:nr * 256], cvf[i][:, 0:nr * 256], AF.Copy), reads=[f'cvf{i}'], writes=[f'cvb{i}'])
                                else:
                                    S.op(eng, lambda e: e.tensor_copy(cvb[i][:, 0:nr * 256], cvf[i][:, 0:nr * 256]), reads=[f'cvf{i}'], writes=[f'cvb{i}'])
                                S.dma(dst3[:, r0_ * 256:(r0_ + nr) * 256], cvb[i][:, 0:nr * 256], reads=[f'cvb{i}'], writes=[f'{nm}in{l}_{k}'])
                            if ncores > 1:
                                S.collective(lambda e, ci=ci, cf=cf, l=l, k=k: e.collective_compute("AllGather", ALU.bypass, replica_groups=[list(range(ncores))],
                                                                                                  ins=[ci[l][k]], outs=[cf[l][k]]),
                                             reads=[f'{nm}in{l}_{k}'], writes=[f'{nm}full{l}'])
                S.barrier()
            bt127 = sb("bt127", [128, 64], BF16)
            btnew = sb("btnew", [8, 64], BF16)
            bclast = sb("bclast", [128, 64], BF16)
            bwin = sb("bwin", [128, 4, 64], BF16)
            b31row = sb("b31row", [1, 64], BF16)
            d8 = sb("d8", [8, 32], BF16)
            fv_s = sb("fv_s", [8, 257], F32)
            vis_s = sb("vis_s", [8, 257], F32)
            m33 = sb("m33", [128, 33], BF16)
            S.dma(fv_s[:], fvs_in, writes=['fv_s'])
            S.dma(vis_s[:], viss_in, writes=['vis_s'])
            with ExitStack() as tst:
                d8f = sb("d8f", [8, 32], F32, tst)
                m33f = sb("m33f", [128, 33], F32, tst)
                b31f = sb("b31f", [1, 8], F32, tst)
                bwa = sb("bwa", [128, 4, 64], BF16, tst)
                bwb = sb("bwb", [128, 4, 64], BF16, tst)
                S.dma(d8f[:], d8_in, writes=['d8f'])
                S.op('dve', lambda e: e.tensor_copy(d8[:], d8f[:]), reads=['d8f'], writes=['d8'])
                S.dma(m33f[:], m33_in, writes=['m33f'])
                S.op('dve', lambda e: e.tensor_copy(m33[:], m33f[:]), reads=['m33f'], writes=['m33'])
                S.dma(b31f[:], rel_bias_in[31:32, :], writes=['b31f'])
                S.op('dve', lambda e: e.tensor_copy(b31row[0:1, :].rearrange("p (h t) -> p h t", t=8), bcast(b31f[0:1, :], [[1, 8], [0, 8]])),
                     reads=['b31f'], writes=['b31row'])
                for hh in range(8):
                    hbase = hh * 128 * NV + OFFV
                    S.dma(bt127[:, hh * 8:(hh + 1) * 8], bass.AP(rep.tensor, hbase + 128, [[NV - 1, 128], [1, 8]]), writes=['bt127'])
                    S.dma(btnew[:, hh * 8:(hh + 1) * 8], bass.AP(rep.tensor, hbase, [[NV - 1, 8], [1, 8]]), writes=['btnew'])
                    S.dma(bclast[0:127, hh * 8:(hh + 1) * 8], bass.AP(rep.tensor, hbase + 2017, [[NV - 16, 127], [1, 8]]), writes=['bclast'])
                    for i in range(4):
                        S.dma(bwa[:, i, hh * 8:(hh + 1) * 8], bass.AP(rep.tensor, hbase + 512 - 128 * i, [[NV - 1, 128], [1, 8]]), writes=['bwa'])
                        S.dma(bwb[:, i, hh * 8:(hh + 1) * 8], bass.AP(rep.tensor, 8 * 128 * NV + OFFV + 512 - 128 * i, [[NV - 1, 128], [1, 8]]), writes=['bwb'])
                S.op('dve', lambda e: e.tensor_tensor(bwin[:], bwa[:], bwb[:], ALU.add), reads=['bwa', 'bwb'], writes=['bwin'])
                S.barrier()

        xT = sb("xT", [128, 8, T], BF16)
        sT = sb("sT", [128, 4, T], BF16)

        WCOLS = 256
        wst = [sb(f"wst{i}", [128, 8, WCOLS], F32) for i in range(2)]
        wbf = [sb(f"wbf{i}", [128, 8, WCOLS], BF16) for i in range(3)]
        w_rr = [0, 0]

        def load_w(src2d, nrows, ncols):
            nk = nrows // 128
            i = w_rr[0] % 2
            w_rr[0] += 1
            j = w_rr[1] % 3
            w_rr[1] += 1
            S.dma(wst[i][:, 0:nk, 0:ncols], src2d.rearrange("(k p) c -> p k c", p=128),
                  writes=[f'wst{i}'])
            S.op('pool', lambda e: e.tensor_copy(wbf[j][:, 0:nk, 0:ncols], wst[i][:, 0:nk, 0:ncols]),
                 reads=[f'wst{i}'], writes=[f'wbf{j}'])
            return wbf[j], f'wbf{j}'

        def proj_fm(wt, wname, col0, M, nk, rhs_fn, rhs_names, evac):
            for c in range(NCH):
                p, pn = next_ps()
                for k in range(nk):
                    S.op('pe', lambda e, k=k: e.matmul(p[0:M, :], wt[:, k, col0:col0 + M], rhs_fn(k, c),
                                                      start=(k == 0), stop=(k == nk - 1)),
                         reads=[wname] + rhs_names, writes=[pn], pe_accum=(k > 0))
                evac(c, p, pn)

        s5c = {}
        for nm, shp, dt_ in (("are", [128, 16], F32), ("aim", [128, 16], F32), ("ldt", [128, 16], F32), ("rho", [128, 16], F32),
                             ("r1", [128, 16], F32), ("tmp1", [128, 16], F32), ("tmp2", [128, 16], F32), ("tmp3", [128, 16], F32),
                             ("kre", [128, 16], F32), ("kim", [128, 16], F32), ("abre", [128, 16], F32), ("abim", [128, 16], F32),
                             ("LBre", [128, 16, 128], BF16), ("LBim", [128, 16, 128], BF16),
                             ("LCre", [128, 16, 128], BF16), ("LCren", [128, 16, 128], BF16), ("LCimn", [128, 16, 128], BF16),
                             ("dcol", [128, 4], F32), ("glub", [128, 8], F32)):
            s5c[nm] = sb("s5_" + nm, shp, dt_)

        def s5_layer_consts(l):
            with ExitStack() as tst:
                for nm, shp, dt_ in (("bre", [128, 256], F32), ("bim", [128, 256], F32), ("cre", [128, 256], F32), ("cim", [128, 256], F32),
                                     ("bbre", [128, 256], F32), ("bbim", [128, 256], F32), ("btmp", [128, 256], F32),
                                     ("xpad", [128, 4, 128], F32)):
                    s5c[nm] = sb("s5_" + nm, shp, dt_, tst)
                s5_layer_consts_body(l)
                S.barrier()

        def s5_layer_consts_body(l):
            c = s5c
            V = lambda f, **kw: S.op('dve', f, **kw)
            for nm, src in (("are", s5_are_in), ("aim", s5_aim_in), ("ldt", s5_ldt_in), ("bre", s5_bre_in), ("bim", s5_bim_in),
                            ("cre", s5_cre_in), ("cim", s5_cim_in), ("dcol", s5_d_in), ("glub", glu_bc_in)):
                S.dma(c[nm][:], src[l], writes=[nm])
            S.op('act', lambda e: e.activation(c["tmp1"][:], c["ldt"][:], AF.Exp), reads=["ldt"], writes=["tmp1"])
            V(lambda e: e.tensor_tensor(c["tmp2"][:], c["tmp1"][:], c["are"][:], ALU.mult), reads=["tmp1", "are"], writes=["tmp2"])
            S.op('act', lambda e: e.activation(c["rho"][:], c["tmp2"][:], AF.Exp), reads=["tmp2"], writes=["rho"])
            V(lambda e: e.tensor_tensor(c["r1"][:], c["tmp1"][:], c["aim"][:], ALU.mult), reads=["tmp1", "aim"], writes=["r1"])
            V(lambda e: e.tensor_scalar(c["r1"][:], c["r1"][:], 1.0 / TWO_PI, None, ALU.mult), reads=["r1"], writes=["r1"])
            for _ in range(6):
                V(lambda e: e.tensor_scalar(c["tmp2"][:], c["r1"][:], 1.0, None, ALU.is_ge), reads=["r1"], writes=["tmp2"])
                V(lambda e: e.tensor_tensor(c["r1"][:], c["r1"][:], c["tmp2"][:], ALU.subtract), reads=["r1", "tmp2"], writes=["r1"])
                V(lambda e: e.tensor_scalar(c["tmp2"][:], c["r1"][:], 0.0, None, ALU.is_lt), reads=["r1"], writes=["tmp2"])
                V(lambda e: e.tensor_tensor(c["r1"][:], c["r1"][:], c["tmp2"][:], ALU.add), reads=["r1", "tmp2"], writes=["r1"])
            S.op('act', lambda e: e.activation(c["tmp2"][:], c["r1"][:], AF.Sin, bias=negpi[:], scale=SIN_SCALE), reads=["r1", "negpi"], writes=["tmp2"])
            V(lambda e: e.tensor_scalar(c["tmp3"][:], c["r1"][:], 0.25, None, ALU.add), reads=["r1"], writes=["tmp3"])
            V(lambda e: e.tensor_scalar(c["tmp1"][:], c["tmp3"][:], 1.0, None, ALU.is_ge), reads=["tmp3"], writes=["tmp1"])
            V(lambda e: e.tensor_tensor(c["tmp3"][:], c["tmp3"][:], c["tmp1"][:], ALU.subtract), reads=["tmp3", "tmp1"], writes=["tmp3"])
            S.op('act', lambda e: e.activation(c["tmp3"][:], c["tmp3"][:], AF.Sin, bias=negpi[:], scale=SIN_SCALE), reads=["tmp3", "negpi"], writes=["tmp3"])
            V(lambda e: e.tensor_tensor(c["tmp3"][:], c["tmp3"][:], c["rho"][:], ALU.mult), reads=["tmp3", "rho"], writes=["tmp3"])
            V(lambda e: e.tensor_scalar(c["tmp3"][:], c["tmp3"][:], -1.0, -1.0, ALU.mult, ALU.add), reads=["tmp3"], writes=["tmp3"])
            V(lambda e: e.tensor_tensor(c["tmp2"][:], c["tmp2"][:], c["rho"][:], ALU.mult), reads=["tmp2", "rho"], writes=["tmp2"])
            V(lambda e: e.tensor_scalar(c["tmp2"][:], c["tmp2"][:], -1.0, None, ALU.mult), reads=["tmp2"], writes=["tmp2"])
            V(lambda e: e.tensor_scalar(c["abre"][:], c["tmp3"][:], 1.0, None, ALU.add), reads=["tmp3"], writes=["abre"])
            V(lambda e: e.tensor_copy(c["abim"][:], c["tmp2"][:]), reads=["tmp2"], writes=["abim"])
            V(lambda e: e.tensor_tensor(c["tmp1"][:], c["are"][:], c["are"][:], ALU.mult), reads=["are"], writes=["tmp1"])
            V(lambda e: e.tensor_tensor(c["kre"][:], c["aim"][:], c["aim"][:], ALU.mult), reads=["aim"], writes=["kre"])
            V(lambda e: e.tensor_tensor(c["tmp1"][:], c["tmp1"][:], c["kre"][:], ALU.add), reads=["tmp1", "kre"], writes=["tmp1"])
            V(lambda e: e.reciprocal(c["tmp1"][:], c["tmp1"][:]), reads=["tmp1"], writes=["tmp1"])
            V(lambda e: e.tensor_tensor(c["kre"][:], c["tmp3"][:], c["are"][:], ALU.mult), reads=["tmp3", "are"], writes=["kre"])
            V(lambda e: e.tensor_tensor(c["kim"][:], c["tmp2"][:], c["aim"][:], ALU.mult), reads=["tmp2", "aim"], writes=["kim"])
            V(lambda e: e.tensor_tensor(c["kre"][:], c["kre"][:], c["kim"][:], ALU.add), reads=["kre", "kim"], writes=["kre"])
            V(lambda e: e.tensor_tensor(c["kre"][:], c["kre"][:], c["tmp1"][:], ALU.mult), reads=["kre", "tmp1"], writes=["kre"])
            V(lambda e: e.tensor_tensor(c["kim"][:], c["tmp2"][:], c["are"][:], ALU.mult), reads=["tmp2", "are"], writes=["kim"])
            V(lambda e: e.tensor_tensor(c["tmp2"][:], c["tmp3"][:], c["aim"][:], ALU.mult), reads=["tmp3", "aim"], writes=["tmp2"])
            V(lambda e: e.tensor_tensor(c["kim"][:], c["kim"][:], c["tmp2"][:], ALU.subtract), reads=["kim", "tmp2"], writes=["kim"])
            V(lambda e: e.tensor_tensor(c["kim"][:], c["kim"][:], c["tmp1"][:], ALU.mult), reads=["kim", "tmp1"], writes=["kim"])
            v3 = lambda t: t[:, :].rearrange("p (j c) -> p j c", c=16)
            kb = lambda t: bcast(t[:, :], [[1, 16], [0, 16]])
            V(lambda e: e.tensor_tensor(v3(c["bbre"]), v3(c["bre"]), kb(c["kre"]), ALU.mult), reads=["bre", "kre"], writes=["bbre"])
            V(lambda e: e.tensor_tensor(v3(c["btmp"]), v3(c["bim"]), kb(c["kim"]), ALU.mult), reads=["bim", "kim"], writes=["btmp"])
            V(lambda e: e.tensor_tensor(c["bbre"][:], c["bbre"][:], c["btmp"][:], ALU.subtract), reads=["bbre", "btmp"], writes=["bbre"])
            V(lambda e: e.tensor_tensor(v3(c["bbim"]), v3(c["bim"]), kb(c["kre"]), ALU.mult), reads=["bim", "kre"], writes=["bbim"])
            V(lambda e: e.tensor_tensor(v3(c["btmp"]), v3(c["bre"]), kb(c["kim"]), ALU.mult), reads=["bre", "kim"], writes=["btmp"])
            V(lambda e: e.tensor_tensor(c["bbim"][:], c["bbim"][:], c["btmp"][:], ALU.add), reads=["bbim", "btmp"], writes=["bbim"])
            mask3 = bdmask[:, :].rearrange("p (g c) -> p g c", c=16)
            for (srcn, dstn) in (("bbre", "LBre"), ("bbim", "LBim")):
                for j4 in range(4):
                    p, pn = next_ps()
                    for jj in range(4):
                        j = j4 * 4 + jj
                        V(lambda e, j=j, jj=jj: e.tensor_tensor(c["xpad"][:, jj, :].rearrange("p (g c) -> p g c", c=16),
                                                                 bcast(c[srcn][:, j * 16:(j + 1) * 16], [[0, 8], [1, 16]]), mask3, ALU.mult),
                          reads=[srcn, 'bdmask'], writes=[f"xpad{jj}"])
                        S.op('pe', lambda e, jj=jj: e.transpose(p[:, jj * 128:(jj + 1) * 128], c["xpad"][:, jj, :], ident_f[:]),
                             reads=[f"xpad{jj}", 'ident_f'], writes=[pn], pe_accum=(jj > 0))
                    S.op('act', lambda e: e.activation(c[dstn][:, j4 * 4:(j4 + 1) * 4, :], p[:, :].rearrange("p (j m) -> p j m", j=4), AF.Copy),
                         reads=[pn], writes=[dstn])
            for j in range(16):
                V(lambda e, j=j: e.tensor_tensor(c["LCre"][:, j, :].rearrange("p (g c) -> p g c", c=16),
                                                 bcast(c["cre"][:, j * 16:(j + 1) * 16], [[0, 8], [1, 16]]), mask3, ALU.mult),
                  reads=["cre", 'bdmask'], writes=["LCre"])
                V(lambda e, j=j: e.tensor_tensor(c["LCimn"][:, j, :].rearrange("p (g c) -> p g c", c=16),
                                                 bcast(c["cim"][:, j * 16:(j + 1) * 16], [[0, 8], [1, 16]]), mask3, ALU.mult),
                  reads=["cim", 'bdmask'], writes=["LCimn"])
            V(lambda e: e.tensor_scalar(c["LCren"][:], c["LCre"][:], -1.0, None, ALU.mult), reads=["LCre"], writes=["LCren"])
            V(lambda e: e.tensor_scalar(c["LCimn"][:], c["LCimn"][:], -1.0, None, ALU.mult), reads=["LCimn"], writes=["LCimn"])

        def sample_pass(l):
            V = lambda f, **kw: S.op('dve', f, **kw)
            G = lambda f, **kw: S.op('pool', f, **kw)
            A = lambda f, **kw: S.op('act', f, **kw)
            P = lambda f, **kw: S.op('pe', f, **kw)
            c5 = s5c
            xs_src = xs_in if l == 0 else ys_out
            with ExitStack() as ph:
                uid[0] += 2
                psA = ph.enter_context(nc.psum_tensor(f"psA_{uid[0]}", [128, 512], F32))
                ptr = ph.enter_context(nc.psum_tensor(f"ptrs_{uid[0] + 1}", [128, 1024], BF16))
                xs_f = sb("xs_f", [NQ, D_MODEL], F32, ph)
                xT_s = sb("xT_s", [128, 8, NQ], BF16, ph)
                qT_s = sb("qT_s", [128, 4, NQ], BF16, ph)
                ksn = sb("ksn", [128, 2, NQ], BF16, ph)
                kwn = sb("kwn", [128, 2, NQ], BF16, ph)
                uT_s = sb("uT_s", [128, 4, NQ], BF16, ph)
                zaT_s = sb("zaT_s", [128, 4, NQ], BF16, ph)
                zsT_s = sb("zsT_s", [128, 4, NQ], BF16, ph)
                gaT_s = sb("gaT_s", [128, 8, NQ], F32, ph)
                gsT_s = sb("gsT_s", [128, 8, NQ], F32, ph)
                sT_s = sb("sT_s", [128, 4, NQ], BF16, ph)
                aT_s = sb("aT_s", [128, 4, NQ], BF16, ph)
                kbs = sb("kbs", [8, NSB, 792], F32, ph)
                gates_s = sb("gates_s", [8, NSB, 24], F32, ph)
                vnew_s = sb("vnew_s", [8, NSB, 2, 65], BF16, ph)
                vnew_w = sb("vnew_w", [8, NSB, 2, 65], BF16, ph)
                o_s = sb("o_s", [8, NSB, 512], F32, ph)
                S.dma(xs_f[:], xs_src, writes=['xs_f'])
                for hb in range(2):
                    p, pn = next_ps()
                    for k4 in range(4):
                        k = hb * 4 + k4
                        P(lambda e, k=k, k4=k4: e.transpose(p[:, k4 * NQ:(k4 + 1) * NQ], xs_f[0:NQ, k * 128:(k + 1) * 128], ident_f[0:NQ, 0:NQ]),
                          reads=['xs_f', 'ident_f'], writes=[pn], pe_accum=(k4 > 0))
                    V(lambda e: e.tensor_copy(xT_s[:, hb * 4:hb * 4 + 4, :], p[:, 0:4 * NQ].rearrange("p (k t) -> p k t", k=4)), reads=[pn], writes=['xT_s'])

                def proj_s(wt, wn, col0, M, evac):
                    p, pn = next_ps()
                    for k in range(8):
                        P(lambda e, k=k: e.matmul(p[0:M, 0:NQ], wt[:, k, col0:col0 + M], xT_s[:, k, :], start=(k == 0), stop=(k == 7)),
                          reads=[wn, 'xT_s'], writes=[pn], pe_accum=(k > 0))
                    evac(p, pn)
                for hp in range(4):
                    wt, wn = load_w(w_in[l, :, C_Q + hp * 128:C_Q + (hp + 1) * 128], D_MODEL, 128)
                    proj_s(wt, wn, 0, 128, lambda p, pn, hp=hp: A(lambda e: e.activation(qT_s[:, hp, :], p[:, 0:NQ], AF.Copy, scale=0.125), reads=[pn], writes=['qT_s']))
                with ExitStack() as ph1:
                    wd = sb("wd_s", [128, 8, 128], BF16, ph1)
                    for (dst, dn, c0) in ((ksn, 'ksn', C_KV + 256), (kwn, 'kwn', C_KV + 512)):
                        wt, wn = load_w(w_in[l, :, c0:c0 + 128], D_MODEL, 128)
                        for g in range(2):
                            G(lambda e, g=g, wt=wt: e.tensor_copy(wd[:, :, 0:64], wt[:, :, 64 * g:64 * g + 64]), reads=[wn], writes=['wd_s'])
                            G(lambda e, g=g, wt=wt: e.tensor_copy(wd[:, :, 64:128], wt[:, :, 64 * g:64 * g + 64]), reads=[wn], writes=['wd_s'])
                            proj_s(wd, 'wd_s', 0, 128, lambda p, pn, g=g, dst=dst, dn=dn: V(lambda e: e.tensor_copy(dst[:, g, :], p[:, 0:NQ]), reads=[pn], writes=[dn]))
                    S.barrier()
                for fb in range(4):
                    wt, wn = load_w(w_in[l, :, C_U + fb * 128:C_U + (fb + 1) * 128], D_MODEL, 128)
                    proj_s(wt, wn, 0, 128, lambda p, pn, fb=fb: A(lambda e: e.activation(uT_s[:, fb, :], p[:, 0:NQ], AF.Copy), reads=[pn], writes=['uT_s']))
                    wt, wn = load_w(w_in[l, :, C_ZA + fb * 128:C_ZA + (fb + 1) * 128], D_MODEL, 128)
                    proj_s(wt, wn, 0, 128, lambda p, pn, fb=fb: A(lambda e: e.activation(zaT_s[:, fb, :], p[:, 0:NQ], AF.Silu), reads=[pn], writes=['zaT_s']))
                    wt, wn = load_w(w_in[l, :, C_ZS + fb * 128:C_ZS + (fb + 1) * 128], D_MODEL, 128)
                    proj_s(wt, wn, 0, 128, lambda p, pn, fb=fb: A(lambda e: e.activation(zsT_s[:, fb, :], p[:, 0:NQ], AF.Silu), reads=[pn], writes=['zsT_s']))
                for blk in range(8):
                    wt, wn = load_w(w_in[l, :, C_GM + blk * 128:C_GM + (blk + 1) * 128], D_MODEL, 128)
                    proj_s(wt, wn, 0, 128, lambda p, pn, blk=blk: A(lambda e: e.activation(gaT_s[:, blk, :], p[:, 0:NQ], AF.Sigmoid), reads=[pn], writes=['gaT_s']))
                    wt, wn = load_w(w_in[l, :, C_GM + 1024 + blk * 128:C_GM + 1024 + (blk + 1) * 128], D_MODEL, 128)
                    proj_s(wt, wn, 0, 128, lambda p, pn, blk=blk: A(lambda e: e.activation(gsT_s[:, blk, :], p[:, 0:NQ], AF.Sigmoid), reads=[pn], writes=['gsT_s']))
                with ExitStack() as ph1:
                    wkv = sb("wkv_s", [128, 8, 792], BF16, ph1)
                    for cb in range(4):
                        cw = 256 if cb < 3 else 24
                        wt, wn = load_w(w_in[l, :, C_KV + cb * 256:C_KV + cb * 256 + cw], D_MODEL, cw)
                        G(lambda e, cb=cb, cw=cw, wt=wt: e.tensor_copy(wkv[:, :, cb * 256:cb * 256 + cw], wt[:, :, 0:cw]), reads=[wn], writes=['wkv_s'])
                    G(lambda e: e.memset(vnew_s[:, :, :, 64:65], 1.0), writes=['vnew_s'])
                    G(lambda e: e.memset(vnew_w[:, :, :, 64:65], 1.0), writes=['vnew_w'])
                    for b in range(NSB):
                        for (c0, cw) in ((0, 512), (512, 280)):
                            p, pn = next_ps()
                            for k in range(8):
                                P(lambda e, k=k: e.matmul(p[0:8, 0:cw], xT_s[:, k, b * 8:(b + 1) * 8], wkv[:, k, c0:c0 + cw], start=(k == 0), stop=(k == 7)),
                                  reads=['xT_s', 'wkv_s'], writes=[pn], pe_accum=(k > 0))
                            V(lambda e: e.tensor_copy(kbs[:, b, c0:c0 + cw], p[0:8, 0:cw]), reads=[pn], writes=['kbs'])
                        S.dma(skvc_out[l, b], kbs[:, b, 0:256], reads=['kbs'])
                        S.dma(skvs_out[l, b], kbs[:, b, 256:512], reads=['kbs'])
                        S.dma(swin_out[l, b, 504:512, :], kbs[:, b, 512:768], reads=['kbs'])
                        S.dma(swin_out[l, b, 0:504, :], swin_in[l, b, 8:512, :])
                        G(lambda e: e.tensor_copy(vnew_s[:, b, :, 0:64], kbs[:, b, 384:512].rearrange("p (g d) -> p g d", g=2)), reads=['kbs'], writes=['vnew_s'])
                        G(lambda e: e.tensor_copy(vnew_w[:, b, :, 0:64], kbs[:, b, 640:768].rearrange("p (g d) -> p g d", g=2)), reads=['kbs'], writes=['vnew_w'])
                        A(lambda e: e.activation(gates_s[:, b, :], kbs[:, b, 768:792], AF.Sigmoid), reads=['kbs'], writes=['gates_s'])
                    S.barrier()
                with ExitStack() as ph1:
                    Bs = sb("Bs", [128, 2, 16, NQ], F32, ph1)
                    Hall = sb("Hall", [128, 2, 16, NQ], F32, ph1)
                    Hb = sb("Hb", [128, 2, 16, NQ], BF16, ph1)
                    h0 = sb("h0", [128, 2, 16 * NSB], F32, ph1)
                    tq_ = [sb(f"tq{i}", [128, 16, NSB], F32, ph1) for i in range(4)]
                    ysb = sb("ysb", [128, 4, NQ], F32, ph1)
                    gy_s = sb("gy_s", [128, 4, NQ], BF16, ph1)
                    gtmp = sb("gtmp", [128, 4, NQ], F32, ph1)
                    hfin = sb("hfin_s", [128, 2, 16], F32, ph1)
                    hft = sb("hft_s", [16, 2, 128], F32, ph1)
                    sgs = sb("sgs", [128, NQ], F32, ph1)
                    S.dma(h0[:, 0, :], h0re_in[l], writes=['h0'])
                    S.dma(h0[:, 1, :], h0im_in[l], writes=['h0'])
                    for ri, ln_ in enumerate(("LBre", "LBim")):
                        p, pn = next_ps()
                        for j in range(16):
                            P(lambda e, j=j, ln_=ln_: e.matmul(p[:, j * NQ:(j + 1) * NQ], c5[ln_][:, j, :], uT_s[:, j // 4, :], start=True, stop=True, skip_group_check=True),
                              reads=[ln_, 'uT_s'], writes=[pn], pe_accum=(j > 0))
                        V(lambda e, ri=ri: e.tensor_copy(Bs[:, ri, :, :], p[:, 0:16 * NQ].rearrange("p (j q) -> p j q", j=16)), reads=[pn], writes=['Bs'])
                    abr = bcast(c5["abre"][:, :], [[1, 16], [0, NSB]])
                    abi = bcast(c5["abim"][:, :], [[1, 16], [0, NSB]])
                    h4 = lambda t_, ri, t: t_[:, ri, :, :].rearrange("p j (b t) -> p j b t", t=8)[:, :, :, t]
                    for t in range(8):
                        if t == 0:
                            pre_ = h0[:, 0, :].rearrange("p (j b) -> p j b", b=NSB)
                            pim_ = h0[:, 1, :].rearrange("p (j b) -> p j b", b=NSB)
                        else:
                            pre_ = h4(Hall, 0, t - 1)
                            pim_ = h4(Hall, 1, t - 1)
                        V(lambda e: e.tensor_tensor(tq_[0][:], pre_, abr, ALU.mult), reads=['Hall', 'h0', 'abre'], writes=['tq0'])
                        V(lambda e: e.tensor_tensor(tq_[1][:], pim_, abi, ALU.mult), reads=['Hall', 'h0', 'abim'], writes=['tq1'])
                        G(lambda e: e.tensor_tensor(tq_[2][:], pim_, abr, ALU.mult), reads=['Hall', 'h0', 'abre'], writes=['tq2'])
                        G(lambda e: e.tensor_tensor(tq_[3][:], pre_, abi, ALU.mult), reads=['Hall', 'h0', 'abim'], writes=['tq3'])
                        V(lambda e: e.tensor_tensor(tq_[0][:], tq_[0][:], tq_[1][:], ALU.subtract), reads=['tq0', 'tq1'], writes=['tq0'])
                        G(lambda e: e.tensor_tensor(tq_[2][:], tq_[2][:], tq_[3][:], ALU.add), reads=['tq2', 'tq3'], writes=['tq2'])
                        V(lambda e: e.tensor_tensor(h4(Hall, 0, t), tq_[0][:], h4(Bs, 0, t), ALU.add), reads=['tq0', 'Bs'], writes=['Hall'])
                        V(lambda e: e.tensor_tensor(h4(Hall, 1, t), tq_[2][:], h4(Bs, 1, t), ALU.add), reads=['tq2', 'Bs'], writes=['Hall'])
                    V(lambda e: e.tensor_copy(Hb[:], Hall[:]), reads=['Hall'], writes=['Hb'])
                    p, pn = next_ps()
                    for J in range(4):
                        for sg in range(4):
                            j = J * 4 + sg
                            P(lambda e, j=j, J=J, sg=sg: e.matmul(p[:, J * NQ:(J + 1) * NQ], c5["LCre"][:, j, :], Hb[:, 0, j, :], start=(J == 0 and sg == 0), stop=False, skip_group_check=True),
                              reads=['LCre', 'Hb'], writes=[pn], pe_accum=not (J == 0 and sg == 0))
                            P(lambda e, j=j, J=J, sg=sg: e.matmul(p[:, J * NQ:(J + 1) * NQ], c5["LCimn"][:, j, :], Hb[:, 1, j, :], start=False, stop=True, skip_group_check=True),
                              reads=['LCimn', 'Hb'], writes=[pn], pe_accum=True)
                    for J in range(4):
                        V(lambda e, J=J: e.scalar_tensor_tensor(ysb[:, J, :], uT_s[:, J, :], c5["dcol"][:, J:J + 1], p[:, J * NQ:(J + 1) * NQ], ALU.mult, ALU.add),
                          reads=['uT_s', 'dcol', pn], writes=['ysb'])
                    G(lambda e: e.tensor_tensor(gtmp[:], ysb[:], ysb[:], ALU.mult), reads=['ysb'], writes=['gtmp'])
                    G(lambda e: e.tensor_scalar(gtmp[:], gtmp[:], 0.044715 * 1.5957691216057308, 1.5957691216057308, ALU.mult, ALU.add), reads=['gtmp'], writes=['gtmp'])
                    G(lambda e: e.tensor_tensor(gtmp[:], gtmp[:], ysb[:], ALU.mult), reads=['gtmp', 'ysb'], writes=['gtmp'])
                    A(lambda e: e.activation(gtmp[:], gtmp[:], AF.Sigmoid), reads=['gtmp'], writes=['gtmp'])
                    G(lambda e: e.tensor_tensor(gy_s[:], gtmp[:], ysb[:], ALU.mult), reads=['gtmp', 'ysb'], writes=['gy_s'])
                    for fb in range(4):
                        w0, w0n = load_w(glu_w_in[l, 0, :, fb * 128:(fb + 1) * 128], 512, 128)
                        w1, w1n = load_w(glu_w_in[l, 1, :, fb * 128:(fb + 1) * 128], 512, 128)
                        pa, pan = next_ps()
                        for kk in range(4):
                            P(lambda e, kk=kk: e.matmul(pa[:, 0:NQ], w0[:, kk, 0:128], gy_s[:, kk, :], start=(kk == 0), stop=(kk == 3)), reads=[w0n, 'gy_s'], writes=[pan], pe_accum=(kk > 0))
                        pb, pbn = next_ps()
                        for kk in range(4):
                            P(lambda e, kk=kk: e.matmul(pb[:, 0:NQ], w1[:, kk, 0:128], gy_s[:, kk, :], start=(kk == 0), stop=(kk == 3)), reads=[w1n, 'gy_s'], writes=[pbn], pe_accum=(kk > 0))
                        A(lambda e: e.activation(sgs[:], pb[:, 0:NQ], AF.Sigmoid, bias=c5["glub"][:, 4 + fb:5 + fb]), reads=[pbn, 'glub'], writes=['sgs'])
                        V(lambda e: e.scalar_tensor_tensor(sgs[:], pa[:, 0:NQ], c5["glub"][:, fb:fb + 1], sgs[:], ALU.add, ALU.mult), reads=[pan, 'glub', 'sgs'], writes=['sgs'])
                        V(lambda e: e.tensor_tensor(sT_s[:, fb, :], sgs[:], zsT_s[:, fb, :], ALU.mult), reads=['sgs', 'zsT_s'], writes=['sT_s'])
                    for b in range(NSB):
                        for ri in range(2):
                            V(lambda e, ri=ri: e.tensor_copy(hfin[:, ri, :], Hall[:, ri, :, b * 8 + 7]), reads=['Hall'], writes=['hfin_s'])
                        p, pn = next_ps()
                        for ri in range(2):
                            P(lambda e, ri=ri: e.transpose(p[0:16, ri * 128:(ri + 1) * 128], hfin[:, ri, :], ident_f[:]), reads=['hfin_s', 'ident_f'], writes=[pn], pe_accum=(ri > 0))
                        V(lambda e: e.tensor_copy(hft[:, :, :], p[0:16, 0:256].rearrange("p (r m) -> p r m", r=2)), reads=[pn], writes=['hft_s'])
                        for ri, dst in enumerate((sssm_re_out, sssm_im_out)):
                            for J in range(4):
                                base = ((l * NSB + b) * 32 + 8 * J) * 64
                                S.dma(bass.AP(dst.tensor, base, [[16, 4], [64, 8], [1, 16]]), hft[4 * J:4 * J + 4, ri, :].rearrange("p (g c) -> p g c", c=16), reads=['hft_s'])
                    S.barrier()
                with ExitStack() as ph1:
                    pg = [sb(f"pg{i}", [128, 256], BF16, ph1) for i in range(2)]
                    pgw = [sb(f"pgw{i}", [128, 256], F32, ph1) for i in range(2)]
                    idx_all = sb("idx_all", [128, NCHK, NSB * PAGES], I32, ph1)
                    with ExitStack() as tst:
                        ptb = sb("ptb", [128, NSB * PAGES], I32, tst)
                        ptf = sb("ptf", [128, NSB * PAGES], F32, tst)
                        io_i = sb("io_i", [128, 1], I32, tst)
                        io_f = sb("io_f", [128, 1], F32, tst)
                        S.dma(ptb[:], bass.AP(pt_in.tensor, 0, [[0, 128], [1, NSB * PAGES]]), writes=['ptb'])
                        S.op('pool', lambda e: e.iota(io_i[:], [[0, 1]], base=0, channel_multiplier=1), writes=['io_i'])
                        S.op('dve', lambda e: e.tensor_copy(io_f[:], io_i[:]), reads=['io_i'], writes=['io_f'])
                        S.op('dve', lambda e: e.tensor_copy(ptf[:], ptb[:]), reads=['ptb'], writes=['ptf'])
                        S.op('dve', lambda e: e.tensor_scalar(ptf[:], ptf[:], 128.0, io_f[:, 0:1], ALU.mult, ALU.add), reads=['ptf', 'io_f'], writes=['ptf'])
                        for k in range(NCHK):
                            S.op('dve', lambda e, k=k: e.tensor_scalar(idx_all[:, k, :], ptf[:], float(-k * ch_rows), None, ALU.add), reads=['ptf'], writes=['idx_all'])
                        S.barrier()
                    dsrc = sb("dsrc", [128, 4, 2, 64], BF16, ph1)
                    XDs = sb("XDs", [128, 4, 2192], BF16, ph1)
                    KT = sb("KTs", [128, 2, 128], BF16, ph1)
                    vpg = sb("vpg", [128, 2, 65], BF16, ph1)
                    w1r = sb("w1r_s", [128, 2, 16, 128], BF16, ph1)
                    posr = sb("posr_s", [128, 16], F32, ph1)
                    posb = sb("posb_s", [128, 16], BF16, ph1)
                    b1c = sb("b1c_s", [128, 2], F32, ph1)
                    b2c = sb("b2c_s", [128, 1], F32, ph1)
                    b2rf = sb("b2rf_s", [1, 64], F32, ph1)
                    b2rb = sb("b2rb_s", [1, 64], BF16, ph1)
                    cvec = sb("cvec_s", [128, 2], F32, ph1)
                    w2f = sb("w2f_s", [128, 64], F32, ph1)
                    w2b = sb("w2b_s", [128, 2, 128], BF16, ph1)
                    sH = sb("sH_s", [128, 128], BF16, ph1)
                    kcmpT = sb("kcmpT_s", [128, 2, 1024], BF16, ph1)
                    vcmp = sb("vcmp_s", [128, 8, 2, 65], BF16, ph1)
                    pts = [sb(f"pts{i}", [128, 32], BF16, ph1) for i in range(3)]
                    pts_rr = [0]
                    imp = sb("imp_s", [8, 2, 257], F32, ph1)
                    score = sb("score_s", [8, 257], F32, ph1)
                    scw = sb("scw_s", [8, 257], F32, ph1)
                    m8 = sb("m8_s", [8, 16], F32, ph1)
                    madd = sb("madd", [8, 2, 260], BF16, ph1)
                    mexp = [sb(f"mexp{i}", [8, 128], BF16, ph1) for i in range(2)]
                    sm = sb("sm_s", [8, 8], F32, ph1)
                    G(lambda e: e.memset(vpg[:, :, 64:65], 1.0), writes=['vpg'])
                    G(lambda e: e.memset(vcmp[:, :, :, 64:65], 1.0), writes=['vcmp_s'])
                    G(lambda e: e.memset(XDs[:], 0.0), writes=['XDs'])
                    G(lambda e: e.memset(madd[:], 0.0), writes=['madd'])
                    S.dma(b1c[:], cmp_b1c_in[l], writes=['b1c_s'])
                    S.dma(b2c[:], cmp_b2c_in[l], writes=['b2c_s'])
                    S.dma(b2rf[:], cmp_b2r_in[l], writes=['b2rf_s'])
                    V(lambda e: e.tensor_copy(b2rb[:], b2rf[:]), reads=['b2rf_s'], writes=['b2rb_s'])
                    for kvi in range(2):
                        for hf in range(2):
                            wt, wn = load_w(cmp_w1_in[l, kvi, hf * 1024:(hf + 1) * 1024, :], 1024, 128)
                            G(lambda e, hf=hf, wt=wt, kvi=kvi: e.tensor_copy(w1r[:, kvi, hf * 8:(hf + 1) * 8, :], wt[:, 0:8, 0:128]), reads=[wn], writes=['w1r_s'])
                        S.dma(posr[:], cmp_posr_in[l, kvi], writes=['posr_s'])
                        V(lambda e: e.tensor_copy(posb[:], posr[:]), reads=['posr_s'], writes=['posb_s'])
                        S.dma(w2f[:], cmp_w2_in[l, kvi], writes=['w2f_s'])
                        V(lambda e, kvi=kvi: e.tensor_copy(w2b[:, kvi, 0:64], w2f[:]), reads=['w2f_s'], writes=['w2b_s'])
                        V(lambda e, kvi=kvi: e.tensor_copy(w2b[:, kvi, 64:128], w2f[:]), reads=['w2f_s'], writes=['w2b_s'])
                        p, pn = next_ps()
                        for i in range(16):
                            P(lambda e, i=i, kvi=kvi: e.matmul(p[:, 0:1], w1r[:, kvi, i, :], posb[:, i:i + 1], start=(i == 0), stop=(i == 15)),
                              reads=['w1r_s', 'posb_s'], writes=[pn], pe_accum=(i > 0))
                        V(lambda e, kvi=kvi: e.tensor_tensor(cvec[:, kvi:kvi + 1], p[:, 0:1], b1c[:, kvi:kvi + 1], ALU.add), reads=[pn, 'b1c_s'], writes=['cvec_s'])

                    def load_page(src_rows, idx_col, reads, k_cols, tag):
                        i = load_page.rr % 2
                        load_page.rr += 1
                        if idx_col is None:
                            S.dma(pgw[i][:], src_rows, reads=reads, writes=[f'pgw{i}'])
                            return pgw[i], f'pgw{i}'
                        else:
                            srcs = [(src_rows[k], idx_all[:, k, idx_col:idx_col + 1], (ch_rows - 1) if NCHK > 1 else None) for k in range(NCHK)]
                            S.idma(pg[i][:, :], srcs, reads=reads + ['idx_all'], writes=[f'pg{i}'])
                        return pg[i], f'pg{i}'
                    load_page.rr = 0

                    def key_tile(b, nkeys, KTt, ktn, vt, vtn, bias, accs, first, use_mask, mask_i):
                        for g in range(2):
                            p, pn = next_ps()
                            for r in range(4):
                                h = 4 * g + r
                                hb = 64 * (h % 2)
                                cr = slice(r * 8, (r + 1) * 8)
                                if bias[0] == 'row':
                                    P(lambda e: e.matmul(p[0:nkeys, cr], ones_row[0:1, 0:nkeys], b31row[0:1, g * 32 + r * 8:g * 32 + (r + 1) * 8], start=True, stop=False),
                                      reads=['ones_row', 'b31row'], writes=[pn], pe_accum=(r > 0))
                                else:
                                    bt_ = bias[1](g)
                                    P(lambda e: e.matmul(p[0:nkeys, cr], ident_b[0:nkeys, 0:nkeys], bt_[:, r * 8:(r + 1) * 8], start=True, stop=False),
                                      reads=['ident_b'] + bias[2], writes=[pn], pe_accum=(r > 0))
                                if use_mask:
                                    if r == 0:
                                        mx, mxn = mexp[g], f"mexp{g}"
                                        src_ = bass.AP(madd[:].tensor, madd[0:8, g, 2 * mask_i:2 * mask_i + 1].offset, [list(madd[:].ap[0])[:1] + [8], [1, 2], [0, 64]])
                                        G(lambda e: e.tensor_copy(mx[0:8, :].rearrange("p (a c) -> p a c", a=2), src_), reads=['madd'], writes=[mxn])
                                    P(lambda e: e.matmul(p[0:nkeys, cr], mx[0:8, :], d8[0:8, cr], start=False, stop=False), reads=[mxn, 'd8'], writes=[pn], pe_accum=True)
                                P(lambda e: e.matmul(p[0:nkeys, cr], KTt[hb:hb + 64, g, 0:nkeys], qT_s[hb:hb + 64, h // 2, b * 8:(b + 1) * 8], start=False, stop=True),
                                  reads=[ktn, 'qT_s'], writes=[pn], pe_accum=True)
                            pt_, ptn = pts[pts_rr[0] % 3], f"pts{pts_rr[0] % 3}"
                            pts_rr[0] += 1
                            A(lambda e: e.activation(pt_[0:nkeys, :], p[0:nkeys, 0:32], AF.Exp), reads=[pn], writes=[ptn])
                            acc, accn = accs[g]
                            for r in range(4):
                                P(lambda e, r=r: e.matmul(acc[0:8, r * 65:(r + 1) * 65], pt_[0:nkeys, r * 8:(r + 1) * 8], vt(g), start=(first and r == 0), stop=True, skip_group_check=True),
                                  reads=[ptn, vtn], writes=[accn], pe_accum=not (first and r == 0))

                    def page_to_KT(pgt, pgn, k0):
                        A(lambda e: e.activation(dsrc[:, 0:2, 0, :], pgt[:, k0:k0 + 128].rearrange("p (g d) -> p g d", g=2), AF.Copy), reads=[pgn], writes=['dsrc'])
                        V(lambda e: e.tensor_copy(dsrc[:, 0:2, 1, :], pgt[:, k0:k0 + 128].rearrange("p (g d) -> p g d", g=2)), reads=[pgn], writes=['dsrc'])
                        G(lambda e: e.tensor_copy(vpg[:, :, 0:64], pgt[:, k0 + 128:k0 + 256].rearrange("p (g d) -> p g d", g=2)), reads=[pgn], writes=['vpg'])
                        for g in range(2):
                            P(lambda e, g=g: e.transpose(ptr[:, g * 128:(g + 1) * 128], dsrc[:, g, :, :].rearrange("p a d -> p (a d)"), ident_b[:]),
                              reads=['dsrc', 'ident_b'], writes=['ptrs'], pe_accum=(g > 0))
                        V(lambda e: e.tensor_copy(KT[:, :, :], ptr[:, 0:256].rearrange("p (g k) -> p g k", g=2)), reads=['ptrs'], writes=['KTs'])

                    for b in range(NSB):
                        for seg in range(8):
                            npg = 17 if seg < 7 else 16
                            ncm = 128 if seg < 7 else 127
                            for pi in range(npg):
                                lp = seg * 16 + pi
                                pgt, pgn = load_page(cfull[l], b * PAGES + lp, [f'cfull{l}'] if ncores > 1 else [], 0, 'c')
                                A(lambda e: e.activation(dsrc[:, :, 0, :], pgt[:, :].rearrange("p (q d) -> p q d", q=4), AF.Copy), reads=[pgn], writes=['dsrc'])
                                V(lambda e: e.tensor_copy(dsrc[:, :, 1, :], pgt[:, :].rearrange("p (q d) -> p q d", q=4)), reads=[pgn], writes=['dsrc'])
                                for q in range(4):
                                    P(lambda e, q=q: e.transpose(ptr[:, q * 128:(q + 1) * 128], dsrc[:, q, :, :].rearrange("p a d -> p (a d)"), ident_b[:]),
                                      reads=['dsrc', 'ident_b'], writes=['ptrs'], pe_accum=(q > 0))
                                r0_ = pi * 128
                                A(lambda e: e.activation(XDs[0:64, :, 1 + r0_:1 + r0_ + 128], ptr[0:64, 0:512].rearrange("p (q k) -> p q k", q=4), AF.Copy), reads=['ptrs'], writes=['XDs'])
                                V(lambda e: e.tensor_copy(XDs[64:128, :, r0_:r0_ + 128], ptr[64:128, 0:512].rearrange("p (q k) -> p q k", q=4)), reads=['ptrs'], writes=['XDs'])
                            for kvi in range(2):
                                for g in range(2):
                                    qx = kvi * 2 + g
                                    p, pn = next_ps()
                                    for i in range(16):
                                        xq = XDs[:, qx, :]
                                        rhs = bass.AP(xq.tensor, xq.offset + 1 + 2 * i, [list(xq.ap[0]), [16, ncm]])
                                        P(lambda e, i=i, rhs=rhs, kvi=kvi: e.matmul(p[:, 0:ncm], w1r[:, kvi, i, :], rhs, start=(i == 0), stop=(i == 15)),
                                          reads=['w1r_s', 'XDs'], writes=[pn], pe_accum=(i > 0))
                                    A(lambda e, kvi=kvi: e.activation(sH[:, 0:ncm], p[:, 0:ncm], AF.Silu, bias=cvec[:, kvi:kvi + 1]), reads=[pn, 'cvec_s'], writes=['sH_s'])
                                    p2, p2n = next_ps()
                                    if kvi == 0:
                                        P(lambda e: e.matmul(p2[:, 0:ncm], w2b[:, 0, :], sH[:, 0:ncm], start=True, stop=True), reads=['w2b_s', 'sH_s'], writes=[p2n])
                                        A(lambda e, g=g: e.activation(kcmpT[:, g, seg * 128:seg * 128 + ncm], p2[:, 0:ncm], AF.Identity, bias=b2c[:]), reads=[p2n, 'b2c_s'], writes=['kcmpT_s'])
                                    else:
                                        P(lambda e: e.matmul(p2[0:ncm, 0:64], sH[:, 0:ncm], w2b[:, 1, 0:64], start=True, stop=False), reads=['w2b_s', 'sH_s'], writes=[p2n])
                                        P(lambda e: e.matmul(p2[0:ncm, 0:64], ones_row[0:1, 0:ncm], b2rb[0:1, :], start=False, stop=True), reads=['ones_row', 'b2rb_s'], writes=[p2n], pe_accum=True)
                                        V(lambda e, g=g: e.tensor_copy(vcmp[0:ncm, seg, g, 0:64], p2[0:ncm, 0:64]), reads=[p2n], writes=['vcmp_s'])
                        accs = [(ps[4], 'ps4'), (ps[5], 'ps5')]
                        for g in range(2):
                            acc, accn = ps[4], 'ps4'
                            imb = [(psA, 'psA'), (ps[5], 'ps5')]
                            for seg in range(8):
                                ncm = 128 if seg < 7 else 127
                                p, pn = next_ps()
                                for r in range(4):
                                    h = 4 * g + r
                                    hb = 64 * (h % 2)
                                    cr = slice(r * 8, (r + 1) * 8)
                                    if seg < 7:
                                        P(lambda e: e.matmul(p[0:ncm, cr], ones_row[0:1, 0:ncm], b31row[0:1, g * 32 + r * 8:g * 32 + (r + 1) * 8], start=True, stop=False),
                                          reads=['ones_row', 'b31row'], writes=[pn], pe_accum=(r > 0))
                                    else:
                                        P(lambda e: e.matmul(p[0:ncm, cr], ident_b[0:ncm, 0:ncm], bclast[0:ncm, g * 32 + r * 8:g * 32 + (r + 1) * 8], start=True, stop=False),
                                          reads=['ident_b', 'bclast'], writes=[pn], pe_accum=(r > 0))
                                    P(lambda e: e.matmul(p[0:ncm, cr], kcmpT[hb:hb + 64, g, seg * 128:seg * 128 + ncm], qT_s[hb:hb + 64, h // 2, b * 8:(b + 1) * 8], start=False, stop=True),
                                      reads=['kcmpT_s', 'qT_s'], writes=[pn], pe_accum=True)
                                pt_, ptn = pts[pts_rr[0] % 3], f"pts{pts_rr[0] % 3}"
                                pts_rr[0] += 1
                                A(lambda e: e.activation(pt_[0:ncm, :], p[0:ncm, 0:32], AF.Exp), reads=[pn], writes=[ptn])
                                nim = 33 if seg < 7 else 32
                                for r in range(4):
                                    P(lambda e, r=r: e.matmul(acc[0:8, r * 65:(r + 1) * 65], pt_[0:ncm, r * 8:(r + 1) * 8], vcmp[0:ncm, seg, g, :], start=(seg == 0 and r == 0), stop=True, skip_group_check=True),
                                      reads=[ptn, 'vcmp_s'], writes=[accn], pe_accum=not (seg == 0 and r == 0))
                                for r in range(4):
                                    ibk, ibkn = imb[r // 2]
                                    c0_ = (r % 2) * 256 + 32 * seg
                                    P(lambda e, r=r, ibk=ibk, c0_=c0_: e.matmul(ibk[0:8, c0_:c0_ + nim], pt_[0:ncm, r * 8:(r + 1) * 8], m33[0:ncm, 0:nim], start=(seg == 0 and r % 2 == 0), stop=True, skip_group_check=True),
                                      reads=[ptn, 'm33'], writes=[ibkn], pe_accum=not (seg == 0 and r % 2 == 0))
                            for r in range(4):
                                h = 4 * g + r
                                ibk, ibkn = imb[r // 2]
                                c0_ = (r % 2) * 256
                                V(lambda e: e.tensor_scalar(sm[:, 0:1], acc[0:8, r * 65 + 64:r * 65 + 65], 1e-30, None, ALU.max), reads=[accn], writes=['sm_s'])
                                V(lambda e: e.reciprocal(sm[:, 0:1], sm[:, 0:1]), reads=['sm_s'], writes=['sm_s'])
                                V(lambda e: e.tensor_tensor(sm[:, 1:2], sm[:, 0:1], gates_s[:, b, 3 * h:3 * h + 1], ALU.mult), reads=['sm_s', 'gates_s'], writes=['sm_s'])
                                V(lambda e: e.tensor_scalar(o_s[:, b, h * 64:(h + 1) * 64], acc[0:8, r * 65:r * 65 + 64], sm[:, 1:2], None, ALU.mult), reads=[accn, 'sm_s'], writes=['o_s'])
                                if 'c' not in DBG_BR:
                                    V(lambda e: e.memset(o_s[:, b, h * 64:(h + 1) * 64], 0.0), writes=['o_s'])
                                if r == 0:
                                    V(lambda e: e.tensor_scalar(imp[:, g, 0:256], ibk[0:8, c0_:c0_ + 256], sm[:, 0:1], None, ALU.mult), reads=[ibkn, 'sm_s'], writes=['imp_s'])
                                else:
                                    V(lambda e: e.scalar_tensor_tensor(imp[:, g, 0:256], ibk[0:8, c0_:c0_ + 256], sm[:, 0:1], imp[:, g, 0:256], ALU.mult, ALU.add), reads=[ibkn, 'sm_s', 'imp_s'], writes=['imp_s'])
                            V(lambda e: e.memset(imp[:, g, 256:257], 0.0), writes=['imp_s'])
                            V(lambda e: e.tensor_tensor(score[:], imp[:, g, :], vis_s[:], ALU.mult), reads=['imp_s', 'vis_s'], writes=['score_s'])
                            V(lambda e: e.tensor_tensor(score[:], score[:], fv_s[:], ALU.add), reads=['score_s', 'fv_s'], writes=['score_s'])
                            V(lambda e: e.max(m8[:, 0:8], score[:]), reads=['score_s'], writes=['m8_s'])
                            V(lambda e: e.match_replace(scw[:], m8[:, 0:8], score[:], -1e9), reads=['score_s', 'm8_s'], writes=['scw_s'])
                            V(lambda e: e.max(m8[:, 8:16], scw[:]), reads=['scw_s'], writes=['m8_s'])
                            V(lambda e: e.tensor_scalar(scw[:], score[:], m8[:, 15:16], 1.0, ALU.is_ge, ALU.subtract), reads=['score_s', 'm8_s'], writes=['scw_s'])
                            V(lambda e: e.tensor_scalar(madd[:, g, 0:257], scw[:], 30000.0, None, ALU.mult), reads=['scw_s'], writes=['madd'])
                        if DBG_OUT and b == 0:
                            S.dma(dbg_madd, madd[:, :, :].rearrange("p a b -> p (a b)"), reads=['madd'])
                            S.dma(dbg_imp, imp[:, :, :].rearrange("p a b -> p (a b)"), reads=['imp_s'])
                        for (branch, ntile) in (("s", PAGES), ("w", 4)):
                            for i in range(ntile):
                                if branch == "s":
                                    pgt, pgn = load_page(sfull[l], b * PAGES + i, [f'sfull{l}'] if ncores > 1 else [], 0, 's')
                                    bias = ('row',) if i < PAGES - 1 else ('tile', lambda g: bt127[:, g * 32:(g + 1) * 32], ['bt127'])
                                else:
                                    pgt, pgn = load_page(swin_in[l, b, i * 128:(i + 1) * 128, :], None, [], 0, 'w')
                                    bias = ('tile', lambda g, i=i: bwin[:, i, g * 32:(g + 1) * 32], ['bwin'])
                                page_to_KT(pgt, pgn, 0)
                                key_tile(b, 128, KT, 'KTs', lambda g: vpg[:, g, :], 'vpg', bias, accs, first=(i == 0), use_mask=(branch == "s"), mask_i=i)
                            knew = ksn if branch == "s" else kwn
                            vnew = vnew_s if branch == "s" else vnew_w
                            kview = bass.AP(knew[:].tensor, knew[:, 0, b * 8:(b + 1) * 8].offset, [list(knew[:].ap[0]), [NQ, 2], [1, 8]])
                            key_tile(b, 8, kview, 'ksn' if branch == "s" else 'kwn', lambda g: vnew[0:8, b, g, :], 'vnew_s' if branch == "s" else 'vnew_w',
                                     ('tile', lambda g: btnew[0:8, g * 32:(g + 1) * 32], ['btnew']), accs, first=False, use_mask=False, mask_i=0)
                            bi = 1 if branch == "s" else 2
                            for g in range(2):
                                acc, accn = accs[g]
                                for r in range(4):
                                    h = 4 * g + r
                                    V(lambda e: e.tensor_scalar(sm[:, 0:1], acc[0:8, r * 65 + 64:r * 65 + 65], 1e-30, None, ALU.max), reads=[accn], writes=['sm_s'])
                                    V(lambda e: e.reciprocal(sm[:, 0:1], sm[:, 0:1]), reads=['sm_s'], writes=['sm_s'])
                                    V(lambda e: e.tensor_tensor(sm[:, 1:2], sm[:, 0:1], gates_s[:, b, 3 * h + bi:3 * h + bi + 1], ALU.mult), reads=['sm_s', 'gates_s'], writes=['sm_s'])
                                    if branch in DBG_BR:
                                        V(lambda e: e.scalar_tensor_tensor(o_s[:, b, h * 64:(h + 1) * 64], acc[0:8, r * 65:r * 65 + 64], sm[:, 1:2], o_s[:, b, h * 64:(h + 1) * 64], ALU.mult, ALU.add),
                                          reads=[accn, 'sm_s', 'o_s'], writes=['o_s'])
                        if DBG_OUT:
                            S.dma(dbg_os[b * 8:(b + 1) * 8, :], o_s[:, b, :], reads=['o_s'])
                        p, pn = next_ps()
                        for fb in range(4):
                            P(lambda e, fb=fb: e.transpose(p[:, fb * 8:(fb + 1) * 8], o_s[0:8, b, fb * 128:(fb + 1) * 128], ident_f[0:8, 0:8]), reads=['o_s', 'ident_f'], writes=[pn], pe_accum=(fb > 0))
                        V(lambda e: e.tensor_tensor(aT_s[:, :, b * 8:(b + 1) * 8], p[:, 0:32].rearrange("p (f t) -> p f t", f=4), zaT_s[:, :, b * 8:(b + 1) * 8], ALU.mult),
                          reads=[pn, 'zaT_s'], writes=['aT_s'])
                    S.barrier()
                with ExitStack() as ph1:
                    mT_s = sb("mT_s", [128, 8, NQ], BF16, ph1)
                    wo2 = sb("wo2_s", [128, 8, 128], BF16, ph1)
                    t1 = sb("t1_s", [128, NQ], F32, ph1)
                    t2 = sb("t2_s", [128, NQ], F32, ph1)
                    for blk in range(8):
                        wa, wan = load_w(w_att_out[l, :, blk * 128:(blk + 1) * 128], 512, 128)
                        G(lambda e, wa=wa: e.tensor_copy(wo2[:, 0:4, :], wa[:, 0:4, 0:128]), reads=[wan], writes=['wo2a_s'])
                        wsx, wsn = load_w(w_ssm_out[l, :, blk * 128:(blk + 1) * 128], 512, 128)
                        G(lambda e, wsx=wsx: e.tensor_copy(wo2[:, 4:8, :], wsx[:, 0:4, 0:128]), reads=[wsn], writes=['wo2s_s'])
                        pa, pan = next_ps()
                        for k in range(4):
                            P(lambda e, k=k: e.matmul(pa[:, 0:NQ], wo2[:, k, :], aT_s[:, k, :], start=(k == 0), stop=(k == 3)), reads=['wo2a_s', 'aT_s'], writes=[pan], pe_accum=(k > 0))
                        V(lambda e: e.tensor_tensor(t1[:], pa[:, 0:NQ], gaT_s[:, blk, :], ALU.mult), reads=[pan, 'gaT_s'], writes=['t1_s'])
                        pss, pssn = next_ps()
                        for k in range(4):
                            P(lambda e, k=k: e.matmul(pss[:, 0:NQ], wo2[:, 4 + k, :], sT_s[:, k, :], start=(k == 0), stop=(k == 3)), reads=['wo2s_s', 'sT_s'], writes=[pssn], pe_accum=(k > 0))
                        V(lambda e: e.tensor_tensor(t2[:], pss[:, 0:NQ], gsT_s[:, blk, :], ALU.mult), reads=[pssn, 'gsT_s'], writes=['t2_s'])
                        V(lambda e: e.tensor_tensor(mT_s[:, blk, :], t1[:], t2[:], ALU.add), reads=['t1_s', 't2_s'], writes=['mT_s'])
                    wob = sb("wob_s", [128, 8, D_MODEL], BF16, ph1)
                    for cb in range(4):
                        wt, wn = load_w(w_o[l, :, cb * 256:(cb + 1) * 256], D_MODEL, 256)
                        G(lambda e, cb=cb, wt=wt: e.tensor_copy(wob[:, :, cb * 256:(cb + 1) * 256], wt[:, :, 0:256]), reads=[wn], writes=['wob_s'])
                    yb = sb("yb_s", [NQ, D_MODEL], F32, ph1)
                    stats = sb("stats_s", [NQ, 2, 6], F32, ph1)
                    mv = sb("mv_s", [NQ, 2], F32, ph1)
                    rstd = sb("rstd_s", [NQ, 1], F32, ph1)
                    for hb in range(2):
                        p, pn = next_ps()
                        for k in range(8):
                            P(lambda e, k=k: e.matmul(p[0:NQ, :], mT_s[:, k, :], wob[:, k, hb * 512:(hb + 1) * 512], start=(k == 0), stop=(k == 7)), reads=['mT_s', 'wob_s'], writes=[pn], pe_accum=(k > 0))
                        V(lambda e: e.scalar_tensor_tensor(yb[:, hb * 512:(hb + 1) * 512], xs_f[:, hb * 512:(hb + 1) * 512], float(ALPHA), p[0:NQ, :], ALU.mult, ALU.add), reads=['xs_f', pn], writes=['yb_s'])
                        V(lambda e: e.bn_stats(stats[:, hb, :], yb[:, hb * 512:(hb + 1) * 512]), reads=['yb_s'], writes=['stats_s'])
                    V(lambda e: e.bn_aggr(mv[:], stats[:, :, :].rearrange("p a b -> p (a b)")), reads=['stats_s'], writes=['mv_s'])
                    A(lambda e: e.activation(rstd[:], mv[:, 1:2], AF.Sqrt, bias=epsc[0:NQ, :], scale=1.0), reads=['mv_s', 'epsc'], writes=['rstd_s'])
                    V(lambda e: e.reciprocal(rstd[:], rstd[:]), reads=['rstd_s'], writes=['rstd_s'])
                    V(lambda e: e.tensor_scalar(yb[:], yb[:], mv[:, 0:1], rstd[:], ALU.subtract, ALU.mult), reads=['yb_s', 'mv_s', 'rstd_s'], writes=['yb_s'])
                    G(lambda e: e.tensor_tensor(yb[:], yb[:], lng[0:NQ, :], ALU.mult), reads=['yb_s', 'lng'], writes=['yb_s'])
                    G(lambda e: e.tensor_tensor(yb[:], yb[:], lnb[0:NQ, :], ALU.add), reads=['yb_s', 'lnb'], writes=['yb_s'])
                    S.dma(ys_out, yb[:], reads=['yb_s'])
                    S.barrier()

        negpi = sb("negpi", [128, 1], F32)
        SIN_SCALE = TWO_PI * (1.0 - 2e-6)
        S.op('dve', lambda e: e.memset(negpi[:], -3.141592653589793 * (1.0 - 2e-6)), writes=['negpi'])

        for l in range(depth):
            S.dma(lng[:, :], bass.AP(ln_g.tensor, l * D_MODEL, [[0, 128], [1, D_MODEL]]), writes=['lng'])
            S.dma(lnb[:, :], bass.AP(ln_b.tensor, l * D_MODEL, [[0, 128], [1, D_MODEL]]), writes=['lnb'])
            s5_layer_consts(l)
            S.barrier()
            for s in range(NS):
                x_src = x_in if (l == 0 or DBG_NOREAD) else xmid
                with ExitStack() as ph:
                    xs = [sb(f"xs{i}", [128, D_MODEL], F32, ph) for i in range(2)]
                    for tt in range(NT):
                        xb = xs[tt % 2]
                        xn = f"xs{tt % 2}"
                        S.dma(xb[:], x_src[s, tt * 128:(tt + 1) * 128, :], writes=[xn])
                        for hb in range(2):
                            p, pn = next_ps()
                            for k4 in range(4):
                                k = hb * 4 + k4
                                S.op('pe', lambda e, k=k, k4=k4: e.transpose(p[:, k4 * 128:(k4 + 1) * 128],
                                                                           xb[:, k * 128:(k + 1) * 128], ident_f[:]),
                                     reads=[xn, 'ident_f'], writes=[pn], pe_accum=(k4 > 0))
                            eng = 'act' if hb == 0 else 'dve'
                            dst = xT[:, hb * 4:hb * 4 + 4, tt * 128:(tt + 1) * 128]
                            src = p[:, :].rearrange("p (k t) -> p k t", k=4)
                            if eng == 'act':
                                S.op('act', lambda e: e.activation(dst, src, AF.Copy), reads=[pn], writes=['xT'])
                            else:
                                S.op('dve', lambda e: e.tensor_copy(dst, src), reads=[pn], writes=['xT'])
                    S.barrier()

                with ExitStack() as ph:
                    c5 = s5c
                    V = lambda f, **kw: S.op('dve', f, **kw)
                    G = lambda f, **kw: S.op('pool', f, **kw)
                    uT = sb("uT", [128, 4, T], BF16, ph)
                    gyT = sb("gyT", [128, 4, T], BF16, ph)
                    for J in range(4):
                        wt, wn = load_w(w_in[l, :, C_U + J * 128:C_U + (J + 1) * 128], D_MODEL, 128)

                        def evac_u(ci, p, pn, J=J):
                            S.op('act', lambda e: e.activation(uT[:, J, ci * 512:(ci + 1) * 512], p[:, :], AF.Copy), reads=[pn], writes=['uT'])
                        proj_fm(wt, wn, 0, 128, 8, lambda k, ci: xT[:, k, ci * 512:(ci + 1) * 512], ['xT'], evac_u)
                    r0 = [sb(f"r0_{i}", [128, 512], F32, ph) for i in range(4)]
                    rhot1 = sb("rhot", [128, 512], F32, ph)
                    offs = sb("offs", [128, 4, 4], F32, ph)
                    wm = sb("wm", [128, 512], F32, ph)
                    rk = sb("rk", [128, 512], F32, ph)
                    rck = sb("rck", [128, 512], F32, ph)
                    cosn = sb("cosn", [128, 512], F32, ph)
                    sinn = sb("sinn", [128, 512], F32, ph)
                    tt_ = [sb(f"t5_{i}", [128, 512], F32, ph) for i in range(4)]
                    bre2 = tt_[1]
                    bim2 = tt_[3]
                    mre = [sb(f"mre_{i}", [128, 512], F32, ph) for i in range(4)]
                    mim = [sb(f"mim_{i}", [128, 512], F32, ph) for i in range(4)]
                    minit = sb("minit", [128, 8], F32, ph)
                    PP = [sb(f"PP_{i}", [128, 512], BF16, ph) for i in range(4)]
                    hfin = sb("hfin", [128, 2, 16], F32, ph)
                    hft = sb("hft", [16, 2, 128], F32, ph)
                    ytmp = sb("ytmp", [128, 512], F32, ph)
                    NK = T // 512
                    for J in range(4):
                        for sg in range(4):
                            j = J * 4 + sg
                            rj = r0[sg]
                            rn = f"r0_{sg}"
                            r1c = c5["r1"][:, j:j + 1]
                            V(lambda e: e.memset(rj[:, 0:1], 0.0), writes=[rn])
                            V(lambda e: e.tensor_copy(rj[:, 1:2], r1c), reads=['r1'], writes=[rn])
                            n = 2
                            while n < 512:
                                V(lambda e, n=n: e.tensor_scalar(wm[:, 0:1], rj[:, n - 1:n], r1c, 1.0, ALU.add, ALU.is_ge), reads=[rn, 'r1'], writes=['wm'])
                                V(lambda e, n=n: e.scalar_tensor_tensor(rj[:, n:n + 1], rj[:, n - 1:n], r1c, wm[:, 0:1], ALU.add, ALU.subtract),
                                  reads=[rn, 'r1', 'wm'], writes=[rn])
                                V(lambda e, n=n: e.tensor_scalar(wm[:, 1:n], rj[:, 1:n], rj[:, n:n + 1], 1.0, ALU.add, ALU.is_ge), reads=[rn], writes=['wm'])
                                V(lambda e, n=n: e.scalar_tensor_tensor(rj[:, n + 1:2 * n], rj[:, 1:n], rj[:, n:n + 1], wm[:, 1:n], ALU.add, ALU.subtract),
                                  reads=[rn, 'wm'], writes=[rn])
                                n *= 2
                            V(lambda e: e.memset(offs[:, sg, 0:1], 0.0), writes=['offs'])
                            V(lambda e: e.tensor_scalar(wm[:, 0:1], rj[:, 511:512], r1c, 1.0, ALU.add, ALU.is_ge), reads=[rn, 'r1'], writes=['wm'])
                            V(lambda e: e.scalar_tensor_tensor(offs[:, sg, 1:2], rj[:, 511:512], r1c, wm[:, 0:1], ALU.add, ALU.subtract),
                              reads=[rn, 'r1', 'wm'], writes=['offs'])
                            for kk in (2, 3):
                                V(lambda e, kk=kk: e.tensor_scalar(wm[:, 0:1], offs[:, sg, kk - 1:kk], offs[:, sg, 1:2], 1.0, ALU.add, ALU.is_ge),
                                  reads=['offs'], writes=['wm'])
                                V(lambda e, kk=kk: e.scalar_tensor_tensor(offs[:, sg, kk:kk + 1], offs[:, sg, kk - 1:kk], offs[:, sg, 1:2], wm[:, 0:1],
                                                                          ALU.add, ALU.subtract), reads=['offs', 'wm'], writes=['offs'])
                        for k in range(NK):
                            cs = slice(k * 512, (k + 1) * 512)
                            yp, ypn = ps[4 + (k % 2)], f"ps{4 + (k % 2)}"
                            for sg in range(4):
                                j = J * 4 + sg
                                rn = f"r0_{sg}"
                                pre, pren = next_ps()
                                S.op('pe', lambda e: e.matmul(pre[:, :], c5["LBre"][:, j, :], uT[:, J, cs], start=True, stop=True),
                                     reads=['LBre', 'uT'], writes=[pren])
                                pim, pimn = next_ps()
                                S.op('pe', lambda e: e.matmul(pim[:, :], c5["LBim"][:, j, :], uT[:, J, cs], start=True, stop=True),
                                     reads=['LBim', 'uT'], writes=[pimn])
                                if k == 0:
                                    rka, rkn = r0[sg], rn
                                else:
                                    V(lambda e: e.tensor_scalar(wm[:], r0[sg][:], offs[:, sg, k:k + 1], 1.0, ALU.add, ALU.is_ge), reads=[rn, 'offs'], writes=['wm'])
                                    V(lambda e: e.scalar_tensor_tensor(rk[:], r0[sg][:], offs[:, sg, k:k + 1], wm[:], ALU.add, ALU.subtract),
                                      reads=[rn, 'offs', 'wm'], writes=['rk'])
                                    rka, rkn = rk, 'rk'
                                V(lambda e: e.tensor_scalar(wm[:], rka[:], 0.25, 1.0, ALU.add, ALU.is_ge), reads=[rkn], writes=['wm'])
                                V(lambda e: e.scalar_tensor_tensor(rck[:], rka[:], 0.25, wm[:], ALU.add, ALU.subtract), reads=[rkn, 'wm'], writes=['rck'])
                                S.op('act', lambda e: e.activation(sinn[:], rka[:], AF.Sin, bias=negpi[:], scale=SIN_SCALE), reads=[rkn, 'negpi'], writes=['sinn'])
                                S.op('act', lambda e: e.activation(cosn[:], rck[:], AF.Sin, bias=negpi[:], scale=SIN_SCALE), reads=['rck', 'negpi'], writes=['cosn'])
                                V(lambda e: e.tensor_tensor(tt_[0][:], pre[:, :], cosn[:], ALU.mult), reads=[pren, 'cosn'], writes=['t5_0'])
                                V(lambda e: e.tensor_tensor(tt_[1][:], pim[:, :], sinn[:], ALU.mult), reads=[pimn, 'sinn'], writes=['t5_1'])
                                V(lambda e: e.tensor_tensor(tt_[2][:], pim[:, :], cosn[:], ALU.mult), reads=[pimn, 'cosn'], writes=['t5_2'])
                                V(lambda e: e.tensor_tensor(tt_[3][:], pre[:, :], sinn[:], ALU.mult), reads=[pren, 'sinn'], writes=['t5_3'])
                                G(lambda e: e.tensor_tensor(bre2[:], tt_[0][:], tt_[1][:], ALU.add), reads=['t5_0', 't5_1'], writes=['t5_1'])
                                G(lambda e: e.tensor_tensor(bim2[:], tt_[2][:], tt_[3][:], ALU.subtract), reads=['t5_2', 't5_3'], writes=['t5_3'])
                                V(lambda e: e.tensor_scalar(rhot1[:], r0[sg][:], 0.0, c5["rho"][:, j:j + 1], ALU.mult, ALU.add), reads=[rn, 'rho'], writes=['rhot'])
                                ini_re = 0.0 if k == 0 else minit[:, sg:sg + 1]
                                ini_im = 0.0 if k == 0 else minit[:, 4 + sg:5 + sg]
                                V(lambda e: e.tensor_tensor_scan(mre[sg][:], rhot1[:], bre2[:], ini_re, ALU.mult, ALU.add),
                                  reads=['rhot', 't5_1', 'minit'], writes=[f"mre_{sg}"])
                                V(lambda e: e.tensor_tensor_scan(mim[sg][:], rhot1[:], bim2[:], ini_im, ALU.mult, ALU.add),
                                  reads=['rhot', 't5_3', 'minit'], writes=[f"mim_{sg}"])
                                if k < NK - 1:
                                    V(lambda e: e.tensor_copy(minit[:, sg:sg + 1], mre[sg][:, 511:512]), reads=[f"mre_{sg}"], writes=['minit'])
                                    V(lambda e: e.tensor_copy(minit[:, 4 + sg:5 + sg], mim[sg][:, 511:512]), reads=[f"mim_{sg}"], writes=['minit'])
                                else:
                                    V(lambda e: e.tensor_tensor(wm[:, 0:1], sinn[:, 511:512], mim[sg][:, 511:512], ALU.mult), reads=['sinn', f"mim_{sg}"], writes=['wm'])
                                    V(lambda e: e.scalar_tensor_tensor(hfin[:, 0, j:j + 1], mre[sg][:, 511:512], cosn[:, 511:512], wm[:, 0:1], ALU.mult, ALU.subtract),
                                      reads=['cosn', f"mre_{sg}", 'wm'], writes=['hfin'])
                                    V(lambda e: e.tensor_tensor(wm[:, 0:1], sinn[:, 511:512], mre[sg][:, 511:512], ALU.mult), reads=['sinn', f"mre_{sg}"], writes=['wm'])
                                    V(lambda e: e.scalar_tensor_tensor(hfin[:, 1, j:j + 1], mim[sg][:, 511:512], cosn[:, 511:512], wm[:, 0:1], ALU.mult, ALU.add),
                                      reads=['cosn', f"mim_{sg}", 'wm'], writes=['hfin'])
                                G(lambda e: e.tensor_tensor(PP[0][:], cosn[:], mre[sg][:], ALU.mult), reads=['cosn', f"mre_{sg}"], writes=['PP_0'])
                                G(lambda e: e.tensor_tensor(PP[1][:], sinn[:], mim[sg][:], ALU.mult), reads=['sinn', f"mim_{sg}"], writes=['PP_1'])
                                G(lambda e: e.tensor_tensor(PP[2][:], cosn[:], mim[sg][:], ALU.mult), reads=['cosn', f"mim_{sg}"], writes=['PP_2'])
                                G(lambda e: e.tensor_tensor(PP[3][:], sinn[:], mre[sg][:], ALU.mult), reads=['sinn', f"mre_{sg}"], writes=['PP_3'])
                                for q, ln_ in enumerate(("LCre", "LCren", "LCimn", "LCimn")):
                                    S.op('pe', lambda e, q=q, ln_=ln_: e.matmul(yp[:, :], c5[ln_][:, j, :], PP[q][:], start=(sg == 0 and q == 0), stop=(sg == 3 and q == 3)),
                                         reads=[ln_, f'PP_{q}'], writes=[ypn], pe_accum=not (sg == 0 and q == 0))
                            V(lambda e: e.scalar_tensor_tensor(ytmp[:], uT[:, J, cs], c5["dcol"][:, J:J + 1], yp[:, :], ALU.mult, ALU.add),
                              reads=['uT', 'dcol', ypn], writes=['ytmp'])
                            G(lambda e: e.tensor_tensor(bre2[:], ytmp[:], ytmp[:], ALU.mult), reads=['ytmp'], writes=['t5_1'])
                            G(lambda e: e.tensor_scalar(bre2[:], bre2[:], 0.044715 * 1.5957691216057308, 1.5957691216057308, ALU.mult, ALU.add), reads=['t5_1'], writes=['t5_1'])
                            G(lambda e: e.tensor_tensor(bre2[:], bre2[:], ytmp[:], ALU.mult), reads=['t5_1', 'ytmp'], writes=['t5_1'])
                            S.op('act', lambda e: e.activation(bre2[:], bre2[:], AF.Sigmoid), reads=['t5_1'], writes=['t5_1'])
                            G(lambda e: e.tensor_tensor(gyT[:, J, cs], bre2[:], ytmp[:], ALU.mult), reads=['t5_1', 'ytmp'], writes=['gyT'])
                    p, pn = next_ps()
                    for ri in range(2):
                        S.op('pe', lambda e, ri=ri: e.transpose(p[0:16, ri * 128:(ri + 1) * 128], hfin[:, ri, :], ident_f[:]),
                             reads=['hfin', 'ident_f'], writes=[pn], pe_accum=(ri > 0))
                    V(lambda e: e.tensor_copy(hft[:, :, :], p[0:16, 0:256].rearrange("p (r m) -> p r m", r=2)), reads=[pn], writes=['hft'])
                    for ri, dst in enumerate((ssm_re_out, ssm_im_out)):
                        for J in range(4):
                            base = ((l * NS + s) * 32 + 8 * J) * 64
                            S.dma(bass.AP(dst.tensor, base, [[16, 4], [64, 8], [1, 16]]),
                                  hft[4 * J:4 * J + 4, ri, :].rearrange("p (g c) -> p g c", c=16), reads=['hft'])
                    sg_t = ytmp
                    for fb in range(4):
                        w0, w0n = load_w(glu_w_in[l, 0, :, fb * 128:(fb + 1) * 128], 512, 128)
                        w1, w1n = load_w(glu_w_in[l, 1, :, fb * 128:(fb + 1) * 128], 512, 128)
                        for k in range(NK):
                            cs = slice(k * 512, (k + 1) * 512)
                            pa, pan = next_ps()
                            for kk in range(4):
                                S.op('pe', lambda e, kk=kk: e.matmul(pa[:, :], w0[:, kk, 0:128], gyT[:, kk, cs], start=(kk == 0), stop=(kk == 3)),
                                     reads=[w0n, 'gyT'], writes=[pan], pe_accum=(kk > 0))
                            pb, pbn = next_ps()
                            for kk in range(4):
                                S.op('pe', lambda e, kk=kk: e.matmul(pb[:, :], w1[:, kk, 0:128], gyT[:, kk, cs], start=(kk == 0), stop=(kk == 3)),
                                     reads=[w1n, 'gyT'], writes=[pbn], pe_accum=(kk > 0))
                            S.op('act', lambda e: e.activation(sg_t[:], pb[:, :], AF.Sigmoid, bias=c5["glub"][:, 4 + fb:5 + fb]), reads=[pbn, 'glub'], writes=['ytmp'])
                            V(lambda e: e.scalar_tensor_tensor(sT[:, fb, cs], pa[:, :], c5["glub"][:, fb:fb + 1], sg_t[:], ALU.add, ALU.mult),
                              reads=[pan, 'glub', 'ytmp'], writes=['sT'])
                    if DBG_OUT:
                        for fb in range(4):
                            S.dma(dbg_s[fb], sT[:, fb, :], reads=['sT'])
                    S.barrier()

                pst = ExitStack()
                o_att = sb("o_att", [128, NT, 512], BF16, pst)
                with ExitStack() as ph:
                    V = lambda f, **kw: S.op('dve', f, **kw)
                    G = lambda f, **kw: S.op('pool', f, **kw)
                    A = lambda f, **kw: S.op('act', f, **kw)
                    P = lambda f, **kw: S.op('pe', f, **kw)
                    uid[0] += 2
                    acc_w = ph.enter_context(nc.psum_tensor(f"accw_{uid[0]}", [128, 512], F32))
                    ptr = ph.enter_context(nc.psum_tensor(f"ptr_{uid[0] + 1}", [128, 1024], BF16))
                    acc_c, acc_cn = ps[4], "ps4"
                    acc_s, acc_sn = ps[5], "ps5"
                    vs_ext = sb("vs_ext", [128, NT, 2, 65], BF16, ph)
                    vw_ext = sb("vw_ext", [128, NT, 2, 65], BF16, ph)
                    gates = sb("gates", [128, NT, 24], F32, ph)
                    XD = [sb(f"XD{i}", [128, T + 16], BF16, ph) for i in range(4)]
                    ksT = sb("ksT", [128, 2, T], BF16, ph)
                    kwT = sb("kwT", [128, 2, T], BF16, ph)
                    kcmpT = sb("kcmpT", [128, 2, 128], BF16, ph)
                    vcmp = sb("vcmp", [128, 2, 97], BF16, ph)
                    G(lambda e: e.memset(vs_ext[:, :, :, 64:65], 1.0), writes=['vs_ext'])
                    G(lambda e: e.memset(vw_ext[:, :, :, 64:65], 1.0), writes=['vw_ext'])
                    for i in range(4):
                        G(lambda e, i=i: e.memset(XD[i][:, :], 0.0), writes=[f'XD{i}'])
                    with ExitStack() as ph1:
                        wkv = sb("wkv", [128, 8, 792], BF16, ph1)
                        for cb in range(4):
                            cw = 256 if cb < 3 else 24
                            wt, wn = load_w(w_in[l, :, C_KV + cb * 256:C_KV + cb * 256 + cw], D_MODEL, cw)
                            G(lambda e, cb=cb, cw=cw, wt=wt: e.tensor_copy(wkv[:, :, cb * 256:cb * 256 + cw], wt[:, :, 0:cw]), reads=[wn], writes=['wkv'])
                        kvt = [sb(f"kvt{i}", [128, 792], F32, ph1) for i in range(2)]
                        for tt in range(NT):
                            kb = kvt[tt % 2]
                            kn = f"kvt{tt % 2}"
                            for (c0, cw) in ((0, 512), (512, 280)):
                                p, pn = next_ps()
                                for k in range(8):
                                    P(lambda e, k=k: e.matmul(p[:, 0:cw], xT[:, k, tt * 128:(tt + 1) * 128], wkv[:, k, c0:c0 + cw], start=(k == 0), stop=(k == 7)),
                                      reads=['xT', 'wkv'], writes=[pn], pe_accum=(k > 0))
                                if c0 == 0:
                                    A(lambda e: e.activation(kb[:, 0:512], p[:, 0:512], AF.Copy), reads=[pn], writes=[kn])
                                else:
                                    V(lambda e: e.tensor_copy(kb[:, 512:792], p[:, 0:280]), reads=[pn], writes=[kn])
                            r0_ = tt * 128
                            S.dma(kvc_out[l, s, r0_:r0_ + 128, :], kb[:, 0:256], reads=[kn])
                            S.dma(kvs_out[l, s, r0_:r0_ + 128, :], kb[:, 256:512], reads=[kn])
                            if r0_ >= T - WIN:
                                w0 = r0_ - (T - WIN)
                                S.dma(kvw_out[l, s, w0:w0 + 128, :], kb[:, 512:768], reads=[kn])
                            G(lambda e: e.tensor_copy(vs_ext[:, tt, :, 0:64], kb[:, 384:512].rearrange("p (g d) -> p g d", g=2)), reads=[kn], writes=['vs_ext'])
                            G(lambda e: e.tensor_copy(vw_ext[:, tt, :, 0:64], kb[:, 640:768].rearrange("p (g d) -> p g d", g=2)), reads=[kn], writes=['vw_ext'])
                            A(lambda e: e.activation(gates[:, tt, :], kb[:, 768:792], AF.Sigmoid), reads=[kn], writes=['gates'])
                        wdup = [sb(f"wdup{i}", [128, 8, 128], BF16, ph1) for i in range(2)]
                        blocks = [("XD", 0, 0), ("XD", 1, 64), ("XD", 2, 128), ("XD", 3, 192),
                                  ("ks", 0, 256), ("ks", 1, 320), ("kw", 0, 512), ("kw", 1, 576)]
                        for bi, (kind, idx, c0) in enumerate(blocks):
                            wd, wdn = wdup[bi % 2], f"wdup{bi % 2}"
                            G(lambda e: e.tensor_copy(wd[:, :, 0:64], wkv[:, :, c0:c0 + 64]), reads=['wkv'], writes=[wdn])
                            G(lambda e: e.tensor_copy(wd[:, :, 64:128], wkv[:, :, c0:c0 + 64]), reads=['wkv'], writes=[wdn])

                            def evac_k(ci, p, pn, kind=kind, idx=idx):
                                if kind == "XD":
                                    A(lambda e: e.activation(XD[idx][0:64, 1 + ci * 512:1 + (ci + 1) * 512], p[0:64, :], AF.Copy), reads=[pn], writes=[f'XD{idx}'])
                                    V(lambda e: e.tensor_copy(XD[idx][64:128, ci * 512:(ci + 1) * 512], p[64:128, :]), reads=[pn], writes=[f'XD{idx}'])
                                elif kind == "ks":
                                    A(lambda e: e.activation(ksT[:, idx, ci * 512:(ci + 1) * 512], p[:, :], AF.Copy), reads=[pn], writes=['ksT'])
                                else:
                                    V(lambda e: e.tensor_copy(kwT[:, idx, ci * 512:(ci + 1) * 512], p[:, :]), reads=[pn], writes=['kwT'])
                            proj_fm(wd, wdn, 0, 128, 8, lambda k, ci: xT[:, k, ci * 512:(ci + 1) * 512], ['xT'], evac_k)
                        S.barrier()
                    if DBG_STAGE < 5:
                        G(lambda e: e.memset(o_att[:], 0.0), writes=['o_att'])
                    with ExitStack() as ph1:
                      if DBG_STAGE >= 2:
                          w1r = sb("w1r", [128, 16, 128], BF16, ph1)
                          posr = sb("posr", [128, 16], F32, ph1)
                          posb = sb("posb", [128, 16], BF16, ph1)
                          b1c = sb("b1c", [128, 2], F32, ph1)
                          b2c = sb("b2c", [128, 1], F32, ph1)
                          b2rf = sb("b2rf", [1, 64], F32, ph1)
                          b2rb = sb("b2rb", [1, 64], BF16, ph1)
                          cvec = sb("cvec", [128, 1], F32, ph1)
                          w2f = sb("w2f", [128, 64], F32, ph1)
                          w2b = sb("w2b", [128, 128], BF16, ph1)
                          sH = sb("sH", [128, 128], BF16, ph1)
                          S.dma(b1c[:], cmp_b1c_in[l], writes=['b1c'])
                          S.dma(b2c[:], cmp_b2c_in[l], writes=['b2c'])
                          S.dma(b2rf[:], cmp_b2r_in[l], writes=['b2rf'])
                          V(lambda e: e.tensor_copy(b2rb[:], b2rf[:]), reads=['b2rf'], writes=['b2rb'])
                          for kvi in range(2):
                              for hf in range(2):
                                  wt, wn = load_w(cmp_w1_in[l, kvi, hf * 1024:(hf + 1) * 1024, :], 1024, 128)
                                  G(lambda e, hf=hf, wt=wt: e.tensor_copy(w1r[:, hf * 8:(hf + 1) * 8, :], wt[:, 0:8, 0:128]), reads=[wn], writes=['w1r'])
                              S.dma(posr[:], cmp_posr_in[l, kvi], writes=['posr'])
                              V(lambda e: e.tensor_copy(posb[:], posr[:]), reads=['posr'], writes=['posb'])
                              S.dma(w2f[:], cmp_w2_in[l, kvi], writes=['w2f'])
                              V(lambda e: e.tensor_copy(w2b[:, 0:64], w2f[:]), reads=['w2f'], writes=['w2b'])
                              V(lambda e: e.tensor_copy(w2b[:, 64:128], w2f[:]), reads=['w2f'], writes=['w2b'])
                              p, pn = next_ps()
                              for i in range(16):
                                  P(lambda e, i=i: e.matmul(p[:, 0:1], w1r[:, i, :], posb[:, i:i + 1], start=(i == 0), stop=(i == 15)),
                                    reads=['w1r', 'posb'], writes=[pn], pe_accum=(i > 0))
                              V(lambda e: e.tensor_tensor(cvec[:], p[:, 0:1], b1c[:, kvi:kvi + 1], ALU.add), reads=[pn, 'b1c'], writes=['cvec'])
                              for g in range(2):
                                  xd = XD[kvi * 2 + g]
                                  xdn = f'XD{kvi * 2 + g}'
                                  p, pn = next_ps()
                                  for i in range(16):
                                      rhs = bass.AP(xd[:].tensor, xd[:].offset + 1 + 2 * i, [list(xd[:].ap[0]), [16, NCMP]])
                                      P(lambda e, i=i, rhs=rhs: e.matmul(p[:, 0:NCMP], w1r[:, i, :], rhs, start=(i == 0), stop=(i == 15)),
                                        reads=['w1r', xdn], writes=[pn], pe_accum=(i > 0))
                                  A(lambda e: e.activation(sH[:, 0:NCMP], p[:, 0:NCMP], AF.Silu, bias=cvec[:]), reads=[pn, 'cvec'], writes=['sH'])
                                  p2, p2n = next_ps()
                                  if kvi == 0:
                                      P(lambda e: e.matmul(p2[:, 0:NCMP], w2b[:, :], sH[:, 0:NCMP], start=True, stop=True), reads=['w2b', 'sH'], writes=[p2n])
                                      A(lambda e: e.activation(kcmpT[:, g, 0:NCMP], p2[:, 0:NCMP], AF.Identity, bias=b2c[:]), reads=[p2n, 'b2c'], writes=['kcmpT'])
                                  else:
                                      P(lambda e: e.matmul(p2[0:NCMP, 0:64], sH[:, 0:NCMP], w2b[:, 0:64], start=True, stop=False), reads=['w2b', 'sH'], writes=[p2n])
                                      P(lambda e: e.matmul(p2[0:NCMP, 0:64], ones_row[0:1, 0:NCMP], b2rb[0:1, :], start=False, stop=True),
                                        reads=['ones_row', 'b2rb'], writes=[p2n], pe_accum=True)
                                      V(lambda e: e.tensor_copy(vcmp[0:NCMP, g, 0:64], p2[0:NCMP, 0:64]), reads=[p2n], writes=['vcmp'])
                          V(lambda e: e.memset(vcmp[:, :, 64:65], 1.0), writes=['vcmp'])
                          for g in range(2):
                              V(lambda e, g=g: e.tensor_copy(vcmp[:, g, 65:97], mimp_b[:, :]), reads=['mimp_b'], writes=['vcmp'])
                          S.barrier()
                    qT = sb("qT", [128, 2, T], BF16, ph)
                    imp = sb("imp", [128, NT, 32], F32, ph)
                    score = sb("score", [128, NT, 32], F32, ph)
                    selT = sb("selT", [32, T], BF16, ph)
                    tbh = [sb(f"tbh{i}", [128, 1024], BF16, ph) for i in range(2)]
                    bct = [sb(f"bct{i}", [128, 512], BF16, ph) for i in range(2)]
                    pT = [sb(f"pT{i}", [128, 512], BF16, ph) for i in range(3)]
                    pT_rr = [0]
                    sm = sb("sm", [128, 16], F32, ph)
                    m8 = sb("m8", [128, 16], F32, ph)
                    scw = sb("scw", [128, 32], F32, ph)
                    selb = sb("selb", [128, 32], BF16, ph)
                    otmp = sb("otmp", [128, 64], F32, ph)
                    for g in (range(2) if DBG_STAGE >= 3 else []):
                        for hp in range(2):
                            wt, wn = load_w(w_in[l, :, C_Q + (2 * g + hp) * 128:C_Q + (2 * g + hp + 1) * 128], D_MODEL, 128)

                            def evac_q(ci, p, pn, hp=hp):
                                A(lambda e: e.activation(qT[:, hp, ci * 512:(ci + 1) * 512], p[:, :], AF.Copy, scale=0.125), reads=[pn], writes=['qT'])
                            proj_fm(wt, wn, 0, 128, 8, lambda k, ci: xT[:, k, ci * 512:(ci + 1) * 512], ['xT'], evac_q)
                        for r in range(4):
                            h = 4 * g + r
                            hb = 64 * (h % 2)
                            hp = r // 2
                            for c in range(NCH):
                                cs = slice(c * 512, (c + 1) * 512)
                                bc_, bcn = bct[c % 2], f"bct{c % 2}"
                                S.dma(bc_[0:NCMP, :], bass.AP(rep.tensor, h * 128 * NV + OFFV + 512 * c - 31, [[NV - 16, NCMP], [1, 512]]), writes=[bcn])
                                p, pn = next_ps()
                                P(lambda e: e.matmul(p[0:NCMP, :], kcmpT[hb:hb + 64, g, 0:NCMP], qT[hb:hb + 64, hp, cs], start=True, stop=False),
                                  reads=['kcmpT', 'qT'], writes=[pn])
                                P(lambda e: e.matmul(p[0:NCMP, :], ident_b[0:NCMP, 0:NCMP], bc_[0:NCMP, :], start=False, stop=True),
                                  reads=['ident_b', bcn], writes=[pn], pe_accum=True)
                                pt, ptn = pT[pT_rr[0] % 3], f"pT{pT_rr[0] % 3}"
                                pT_rr[0] += 1
                                A(lambda e: e.activation(pt[0:NCMP, :], p[0:NCMP, :], AF.Exp), reads=[pn], writes=[ptn])
                                for t4 in range(4):
                                    P(lambda e, t4=t4: e.matmul(acc_c[:, t4 * 97:(t4 + 1) * 97], pt[0:NCMP, t4 * 128:(t4 + 1) * 128], vcmp[0:NCMP, g, :],
                                                                start=True, stop=True, skip_group_check=True),
                                      reads=[ptn, 'vcmp'], writes=[acc_cn], pe_accum=(t4 > 0))
                                for t4 in range(4):
                                    tt = c * 4 + t4
                                    a0 = t4 * 97
                                    V(lambda e: e.tensor_scalar(sm[:, 0:1], acc_c[:, a0 + 64:a0 + 65], 1e-30, None, ALU.max), reads=[acc_cn], writes=['sm'])
                                    V(lambda e: e.reciprocal(sm[:, 0:1], sm[:, 0:1]), reads=['sm'], writes=['sm'])
                                    V(lambda e: e.tensor_tensor(sm[:, 1:2], sm[:, 0:1], gates[:, tt, 3 * h:3 * h + 1], ALU.mult), reads=['sm', 'gates'], writes=['sm'])
                                    V(lambda e: e.tensor_scalar(o_att[:, tt, h * 64:(h + 1) * 64], acc_c[:, a0:a0 + 64], sm[:, 1:2], None, ALU.mult),
                                      reads=[acc_cn, 'sm'], writes=['o_att'])
                                    if r == 0:
                                        V(lambda e: e.tensor_scalar(imp[:, tt, 0:NBLK], acc_c[:, a0 + 65:a0 + 65 + NBLK], sm[:, 0:1], None, ALU.mult),
                                          reads=[acc_cn, 'sm'], writes=['imp'])
                                    else:
                                        V(lambda e: e.scalar_tensor_tensor(imp[:, tt, 0:NBLK], acc_c[:, a0 + 65:a0 + 65 + NBLK], sm[:, 0:1], imp[:, tt, 0:NBLK], ALU.mult, ALU.add),
                                          reads=[acc_cn, 'sm', 'imp'], writes=['imp'])
                        if DBG_STAGE < 4:
                            continue
                        if NBLK > 16:
                            V(lambda e: e.tensor_tensor(score[:], imp[:], vis_mul[:], ALU.mult), reads=['imp', 'vis_mul'], writes=['score'])
                            V(lambda e: e.tensor_tensor(score[:], score[:], fv_add[:], ALU.add), reads=['score', 'fv_add'], writes=['score'])
                            for tt in range(NT):
                                V(lambda e: e.max(m8[:, 0:8], score[:, tt, :]), reads=['score'], writes=['m8'])
                                V(lambda e: e.match_replace(scw[:], m8[:, 0:8], score[:, tt, :], -1e9), reads=['score', 'm8'], writes=['scw'])
                                V(lambda e: e.max(m8[:, 8:16], scw[:]), reads=['scw'], writes=['m8'])
                                V(lambda e: e.tensor_scalar(scw[:], score[:, tt, :], m8[:, 15:16], 1.0, ALU.is_ge, ALU.subtract), reads=['score', 'm8'], writes=['scw'])
                                V(lambda e: e.tensor_scalar(selb[:], scw[:], 30000.0, None, ALU.mult), reads=['scw'], writes=['selb'])
                                P(lambda e: e.transpose(ptr[0:32, (tt % 8) * 128:(tt % 8 + 1) * 128], selb[:, :], ident_b[:]), reads=['selb', 'ident_b'], writes=['ptr'])
                                A(lambda e: e.activation(selT[:, tt * 128:(tt + 1) * 128], ptr[0:32, (tt % 8) * 128:(tt % 8 + 1) * 128], AF.Copy), reads=['ptr'], writes=['selT'])
                        else:
                            V(lambda e: e.memset(selT[:], 0.0), writes=['selT'])
                        for r in (range(4) if DBG_STAGE >= 5 else []):
                            h = 4 * g + r
                            hb = 64 * (h % 2)
                            hp = r // 2
                            tb, tbn = tbh[h % 2], f"tbh{h % 2}"
                            S.dma(tb[:], bass.AP(rep.tensor, h * 128 * NV + OFFV - 384, [[NV - 1, 128], [1, 1024]]), writes=[tbn])
                            for c in range(NCH):
                                cs = slice(c * 512, (c + 1) * 512)
                                for (branch, kT_, v_ext, acc, accn, i_lo) in (("s", ksT, vs_ext, acc_s, acc_sn, 0), ("w", kwT, vw_ext, acc_w, "acc_w", max(0, 4 * c - 4))):
                                    first = True
                                    for i in range(i_lo, 4 * c + 4):
                                        dl = 4 * c - i
                                        x0 = 384 + 128 * dl
                                        p, pn = next_ps()
                                        extra = []
                                        if branch == "s":
                                            extra.append((emat[0:32, i, :], selT[0:32, cs], ['emat', 'selT']))
                                        if dl <= 1:
                                            extra.append((ident_b[:, :], tb[:, x0:x0 + 512], ['ident_b', tbn]))
                                        if branch == "w" and dl >= 1:
                                            extra.append((ident_b[:, :], tbwc[:, x0:x0 + 512], ['ident_b', 'tbwc']))
                                        P(lambda e: e.matmul(p[:, :], kT_[hb:hb + 64, g, i * 128:(i + 1) * 128], qT[hb:hb + 64, hp, cs], start=True, stop=(len(extra) == 0)),
                                          reads=['ksT', 'kwT', 'qT'], writes=[pn])
                                        for ei, (lh, rh, nms) in enumerate(extra):
                                            P(lambda e, lh=lh, rh=rh, ei=ei: e.matmul(p[:, :], lh, rh, start=False, stop=(ei == len(extra) - 1)),
                                              reads=nms, writes=[pn], pe_accum=True)
                                        pt, ptn = pT[pT_rr[0] % 3], f"pT{pT_rr[0] % 3}"
                                        pT_rr[0] += 1
                                        if dl >= 2:
                                            A(lambda e: e.activation(pt[:], p[:, :], AF.Exp, bias=b31col[:, h:h + 1]), reads=[pn, 'b31col'], writes=[ptn])
                                        else:
                                            A(lambda e: e.activation(pt[:], p[:, :], AF.Exp), reads=[pn], writes=[ptn])
                                        for t4 in range(4):
                                            P(lambda e, t4=t4: e.matmul(acc[:, t4 * 65:(t4 + 1) * 65], pt[:, t4 * 128:(t4 + 1) * 128], v_ext[:, i, g, :],
                                                                        start=(first and t4 == 0), stop=True, skip_group_check=True),
                                              reads=[ptn, 'vs_ext', 'vw_ext'], writes=[accn], pe_accum=not (first and t4 == 0))
                                        first = False
                                for t4 in range(4):
                                    tt = c * 4 + t4
                                    a0 = t4 * 65
                                    V(lambda e: e.tensor_scalar(sm[:, 0:1], acc_s[:, a0 + 64:a0 + 65], 1e-30, None, ALU.max), reads=[acc_sn], writes=['sm'])
                                    V(lambda e: e.tensor_scalar(sm[:, 2:3], acc_w[:, a0 + 64:a0 + 65], 1e-30, None, ALU.max), reads=['acc_w'], writes=['sm'])
                                    V(lambda e: e.reciprocal(sm[:, 0:1], sm[:, 0:1]), reads=['sm'], writes=['sm'])
                                    V(lambda e: e.reciprocal(sm[:, 2:3], sm[:, 2:3]), reads=['sm'], writes=['sm'])
                                    V(lambda e: e.tensor_tensor(sm[:, 4:5], sm[:, 0:1], gates[:, tt, 3 * h + 1:3 * h + 2], ALU.mult), reads=['sm', 'gates'], writes=['sm'])
                                    V(lambda e: e.tensor_tensor(sm[:, 5:6], sm[:, 2:3], gates[:, tt, 3 * h + 2:3 * h + 3], ALU.mult), reads=['sm', 'gates'], writes=['sm'])
                                    V(lambda e: e.tensor_scalar(otmp[:], acc_s[:, a0:a0 + 64], sm[:, 4:5], None, ALU.mult), reads=[acc_sn, 'sm'], writes=['otmp'])
                                    V(lambda e: e.scalar_tensor_tensor(otmp[:], acc_w[:, a0:a0 + 64], sm[:, 5:6], otmp[:], ALU.mult, ALU.add), reads=['acc_w', 'sm', 'otmp'], writes=['otmp'])
                                    V(lambda e: e.tensor_tensor(o_att[:, tt, h * 64:(h + 1) * 64], otmp[:], o_att[:, tt, h * 64:(h + 1) * 64], ALU.add),
                                      reads=['otmp', 'o_att'], writes=['o_att'])
                    if DBG_OUT:
                        for tt in range(NT):
                            S.dma(dbg_o[tt * 128:(tt + 1) * 128, :], o_att[:, tt, :], reads=['o_att'])
                    S.barrier()

                with ExitStack() as ph:
                    mT = sb("mT", [128, 8, T], BF16, ph)
                    ph2 = ExitStack()
                    aT = sb("aT", [128, 4, T], BF16, ph2)
                    zt = [sb(f"zt{i}", [128, 512], BF16, ph2) for i in range(2)]
                    uid[0] += 2
                    pbts = [ph.enter_context(nc.psum_tensor(f"pbt_{uid[0] + i}", [128, 1024], BF16)) for i in range(2)]
                    for fb in range(4):
                        wt, wn = load_w(w_in[l, :, C_ZA + fb * 128:C_ZA + (fb + 1) * 128], D_MODEL, 128)

                        def evac_za(c, p, pn, fb=fb):
                            z = zt[c % 2]
                            zn = f"zt{c % 2}"
                            S.op('act', lambda e: e.activation(z[:], p[:, :], AF.Silu), reads=[pn], writes=[zn])
                            half = 0
                            pbt = pbts[c % 2]
                            for t4 in range(4):
                                tt = c * 4 + t4
                                S.op('pe', lambda e, t4=t4, tt=tt: e.transpose(pbt[:, half + t4 * 128:half + (t4 + 1) * 128],
                                                                             o_att[:, tt, fb * 128:(fb + 1) * 128], ident_b[:]),
                                     reads=['o_att', 'ident_b'], writes=[f'pbt{c % 2}'], pe_accum=(t4 > 0))
                            S.op('dve', lambda e: e.tensor_tensor(aT[:, fb, c * 512:(c + 1) * 512], pbt[:, half:half + 512], z[:], ALU.mult),
                                 reads=[f'pbt{c % 2}', zn], writes=['aT'])
                        proj_fm(wt, wn, 0, 128, 8, lambda k, c: xT[:, k, c * 512:(c + 1) * 512], ['xT'], evac_za)
                    for fb in range(4):
                        wt, wn = load_w(w_in[l, :, C_ZS + fb * 128:C_ZS + (fb + 1) * 128], D_MODEL, 128)

                        def evac_zs(c, p, pn, fb=fb):
                            z = zt[c % 2]
                            zn = f"zt{c % 2}"
                            S.op('act', lambda e: e.activation(z[:], p[:, :], AF.Silu), reads=[pn], writes=[zn])
                            S.op('dve', lambda e: e.tensor_tensor(sT[:, fb, c * 512:(c + 1) * 512], sT[:, fb, c * 512:(c + 1) * 512], z[:], ALU.mult),
                                 reads=['sT', zn], writes=['sT'])
                        proj_fm(wt, wn, 0, 128, 8, lambda k, c: xT[:, k, c * 512:(c + 1) * 512], ['xT'], evac_zs)
                    gs = [sb(f"gs{i}", [128, 512], F32, ph2) for i in range(2)]
                    t1 = sb("t1", [128, 512], F32, ph2)
                    for blk in range(8):
                        wo2 = sb(f"wo2", [128, 8, 128], BF16, ph2) if blk == 0 else wo2
                        wa, wan = load_w(w_att_out[l, :, blk * 128:(blk + 1) * 128], 512, 128)
                        S.op('pool', lambda e, wa=wa: e.tensor_copy(wo2[:, 0:4, :], wa[:, 0:4, 0:128]), reads=[wan], writes=['wo2a'])
                        wsx, wsn = load_w(w_ssm_out[l, :, blk * 128:(blk + 1) * 128], 512, 128)
                        S.op('pool', lambda e, wsx=wsx: e.tensor_copy(wo2[:, 4:8, :], wsx[:, 0:4, 0:128]), reads=[wsn], writes=['wo2s'])
                        wga, wgan = load_w(w_in[l, :, C_GM + blk * 128:C_GM + (blk + 1) * 128], D_MODEL, 128)
                        wgs, wgsn = load_w(w_in[l, :, C_GM + 1024 + blk * 128:C_GM + 1024 + (blk + 1) * 128], D_MODEL, 128)
                        for c in range(NCH):
                            cs = slice(c * 512, (c + 1) * 512)
                            pga, pgan = next_ps()
                            for k in range(8):
                                S.op('pe', lambda e, k=k: e.matmul(pga[:, :], wga[:, k, 0:128], xT[:, k, cs], start=(k == 0), stop=(k == 7)),
                                     reads=[wgan, 'xT'], writes=[pgan], pe_accum=(k > 0))
                            S.op('act', lambda e: e.activation(gs[0][:], pga[:, :], AF.Sigmoid), reads=[pgan], writes=['gs0'])
                            pgs, pgsn = next_ps()
                            for k in range(8):
                                S.op('pe', lambda e, k=k: e.matmul(pgs[:, :], wgs[:, k, 0:128], xT[:, k, cs], start=(k == 0), stop=(k == 7)),
                                     reads=[wgsn, 'xT'], writes=[pgsn], pe_accum=(k > 0))
                            S.op('act', lambda e: e.activation(gs[1][:], pgs[:, :], AF.Sigmoid), reads=[pgsn], writes=['gs1'])
                            pa, pan = next_ps()
                            for k in range(4):
                                S.op('pe', lambda e, k=k: e.matmul(pa[:, :], wo2[:, k, :], aT[:, k, cs], start=(k == 0), stop=(k == 3)),
                                     reads=['wo2a', 'aT'], writes=[pan], pe_accum=(k > 0))
                            S.op('dve', lambda e: e.tensor_tensor(t1[:], pa[:, :], gs[0][:], ALU.mult), reads=[pan, 'gs0'], writes=['t1'])
                            pss, pssn = next_ps()
                            for k in range(4):
                                S.op('pe', lambda e, k=k: e.matmul(pss[:, :], wo2[:, 4 + k, :], sT[:, k, cs], start=(k == 0), stop=(k == 3)),
                                     reads=['wo2s', 'sT'], writes=[pssn], pe_accum=(k > 0))
                            S.op('dve', lambda e: e.tensor_tensor(gs[1][:], pss[:, :], gs[1][:], ALU.mult), reads=[pssn, 'gs1'], writes=['gs1'])
                            S.op('dve', lambda e: e.tensor_tensor(mT[:, blk, cs], t1[:], gs[1][:], ALU.add), reads=['t1', 'gs1'], writes=['mT'])
                    S.barrier()
                    ph2.close()
                    wob = sb("wob", [128, 8, D_MODEL], BF16, ph)
                    for cb in range(4):
                        wt, wn = load_w(w_o[l, :, cb * 256:(cb + 1) * 256], D_MODEL, 256)
                        S.op('pool', lambda e, cb=cb, wt=wt: e.tensor_copy(wob[:, :, cb * 256:(cb + 1) * 256], wt[:, :, 0:256]),
                             reads=[wn], writes=['wob'])
                    xr = [sb(f"xr{i}", [128, D_MODEL], F32, ph) for i in range(2)]
                    yt = [sb(f"yt{i}", [128, D_MODEL], F32, ph) for i in range(2)]
                    stats = sb("stats", [128, 2, 6], F32, ph)
                    mv = sb("mv", [128, 2], F32, ph)
                    rstd = sb("rstd", [128, 1], F32, ph)
                    nmr = sb("nmr", [128, 1], F32, ph)
                    for tt in range(NT):
                        xb, xn = xr[tt % 2], f"xr{tt % 2}"
                        yb, yn = yt[tt % 2], f"yt{tt % 2}"
                        S.dma(xb[:], x_src[s, tt * 128:(tt + 1) * 128, :], writes=[xn])
                        for hb in range(2):
                            p, pn = next_ps()
                            for k in range(8):
                                S.op('pe', lambda e, k=k: e.matmul(p[:, :], mT[:, k, tt * 128:(tt + 1) * 128], wob[:, k, hb * 512:(hb + 1) * 512],
                                                                  start=(k == 0), stop=(k == 7)),
                                     reads=['mT', 'wob'], writes=[pn], pe_accum=(k > 0))
                            S.op('dve', lambda e: e.scalar_tensor_tensor(yb[:, hb * 512:(hb + 1) * 512], xb[:, hb * 512:(hb + 1) * 512], float(ALPHA),
                                                                          p[:, :], ALU.mult, ALU.add),
                                 reads=[xn, pn], writes=[yn])
                            S.op('dve', lambda e: e.bn_stats(stats[:, hb, :], yb[:, hb * 512:(hb + 1) * 512]), reads=[yn], writes=['stats'])
                        S.op('dve', lambda e: e.bn_aggr(mv[:], stats[:, :, :].rearrange("p a b -> p (a b)")), reads=['stats'], writes=['mv'])
                        S.op('act', lambda e: e.activation(rstd[:], mv[:, 1:2], AF.Sqrt, bias=epsc[:], scale=1.0), reads=['mv', 'epsc'], writes=['rstd'])
                        S.op('dve', lambda e: e.reciprocal(rstd[:], rstd[:]), reads=['rstd'], writes=['rstd'])
                        S.op('dve', lambda e: e.tensor_scalar(yb[:], yb[:], mv[:, 0:1], rstd[:], ALU.subtract, ALU.mult),
                             reads=[yn, 'mv', 'rstd'], writes=[yn])
                        S.op('pool', lambda e: e.tensor_tensor(yb[:], yb[:], lng[:, :], ALU.mult), reads=[yn, 'lng'], writes=[yn])
                        S.op('pool', lambda e: e.tensor_tensor(yb[:], yb[:], lnb[:, :], ALU.add), reads=[yn, 'lnb'], writes=[yn])
                        dst = y_out if l == depth - 1 else xmid
                        S.dma(dst[s, tt * 128:(tt + 1) * 128, :], yb[:], reads=[yn])
                    S.barrier()
                pst.close()
            if NSB:
                sample_pass(l)
        S.finish()
    return nc


def s5_host_layouts(inputs):
    f32 = np.float32
    L = inputs["ssm_a_re"].shape[0]

    def gs(a):
        a = np.asarray(a, f32).reshape(L, 4, 8, 4, 16)
        return np.ascontiguousarray(a.transpose(0, 2, 4, 1, 3).reshape(L, 128, 16))

    out = {}
    out["s5_are"] = gs(inputs["ssm_a_re"])
    out["s5_aim"] = gs(inputs["ssm_a_im"])
    out["s5_ldt"] = gs(np.broadcast_to(np.asarray(inputs["ssm_log_dt"], f32)[:, :, None], (L, 32, 64)))
    for nm, key in (("s5_bre", "ssm_b_re"), ("s5_bim", "ssm_b_im")):
        b = np.asarray(inputs[key], f32).reshape(L, 4, 8, 4, 16, 16)
        out[nm] = np.ascontiguousarray(b.transpose(0, 2, 4, 1, 3, 5).reshape(L, 128, 256))
    for nm, key in (("s5_cre", "ssm_c_re"), ("s5_cim", "ssm_c_im")):
        c = np.asarray(inputs[key], f32).reshape(L, 4, 8, 16, 4, 16)
        out[nm] = np.ascontiguousarray(c.transpose(0, 2, 5, 1, 4, 3).reshape(L, 128, 256))
    d = np.asarray(inputs["ssm_d"], f32).reshape(L, 4, 8, 16)
    out["s5_d"] = np.ascontiguousarray(d.transpose(0, 2, 3, 1).reshape(L, 128, 4))
    gb = np.asarray(inputs["ssm_glu_b"], f32).reshape(L, 2, 4, 128)
    out["glu_bc"] = np.ascontiguousarray(gb.transpose(0, 3, 1, 2).reshape(L, 128, 8))
    out["glu_w"] = np.ascontiguousarray(inputs["ssm_glu_w"], dtype=f32)
    gi = np.arange(128) // 16
    out["bdmask"] = (gi[:, None] == gi[None, :]).astype(f32)
    return out


def attn_host_consts(T):
    f32 = np.float32
    NT = T // 128
    NV, OFFV = 4608, 2048
    d = np.arange(NV) - OFFV
    n = np.maximum(d, 0)
    nf = np.maximum(n, 1).astype(f32)
    large = 16 + (np.log(nf / f32(16)) / f32(np.log(128 / 16)) * f32(16)).astype(np.int32)
    bucket = np.where(n < 16, n, np.minimum(large, 31))
    oh = np.zeros((33, NV), f32)
    pos = d >= 0
    oh[bucket[pos], np.nonzero(pos)[0]] = 1.0
    oh[32, ~pos] = 1.0
    wcut = np.where(d > 512, -30000.0, 0.0).astype(f32)[None, :]
    emat = np.zeros((32, NT, 128), f32)
    for i in range(NT):
        for tk in range(128):
            j = 2 * i + tk // 64
            if j < 32:
                emat[j, i, tk] = 1.0
    t = (np.arange(NT)[None, :] * 128 + np.arange(128)[:, None])
    cur = t // 64
    j = np.arange(32)[None, None, :]
    forced = (j == 0) | (j == cur[..., None]) | (j == cur[..., None] - 1)
    visible = j <= cur[..., None]
    vis_mul = (visible & ~forced).astype(f32)
    fv_add = np.where(forced, 100.0 + j, np.where(visible, 0.0, -(100.0 + j))).astype(f32)
    ncmp = T // 16 - 1
    nblk = T // 64
    mimp = np.zeros((128, 32), f32)
    for jb in range(nblk):
        for off, wgt in ((-1, 1.0), (0, 2.0), (1, 2.0), (2, 2.0), (3, 1.0)):
            nn = 4 * jb + off
            if 0 <= nn < ncmp:
                mimp[nn, jb] = wgt
    return {"oh_bias": oh, "wcut": wcut, "emat": emat, "fv_add": np.ascontiguousarray(fv_add), "vis_mul": np.ascontiguousarray(vis_mul), "mimp": mimp}


def cmp_host_layouts(inputs):
    f32 = np.float32
    w1 = np.asarray(inputs["cmp_w1"], f32)
    L = w1.shape[0]
    out = {"cmp_w1": np.ascontiguousarray(w1.reshape(L, 2, 2048, 128)),
           "cmp_w2": np.ascontiguousarray(inputs["cmp_w2"], dtype=f32)}
    pos = np.asarray(inputs["cmp_pos"], f32).reshape(L, 2, 16, 2, 64)
    out["cmp_posr"] = np.ascontiguousarray(pos.transpose(0, 1, 3, 4, 2).reshape(L, 2, 128, 16))
    out["cmp_b1c"] = np.ascontiguousarray(np.asarray(inputs["cmp_b1"], f32).transpose(0, 2, 1))
    b2 = np.asarray(inputs["cmp_b2"], f32)
    out["cmp_b2c"] = np.ascontiguousarray(np.concatenate([b2[:, 0], b2[:, 0]], axis=-1)[:, :, None])
    out["cmp_b2r"] = np.ascontiguousarray(b2[:, 1][:, None, :])
    out["rel_bias"] = np.ascontiguousarray(inputs["rel_bias"], dtype=f32)
    return out


def sample_host_consts():
    f32 = np.float32
    d8 = np.zeros((8, 32), f32)
    for r in range(4):
        for t in range(8):
            d8[t, r * 8 + t] = 1.0
    j = np.arange(257)
    forced = (j == 0) | (j == 255) | (j == 256)
    fv = np.broadcast_to(np.where(forced, 100.0 + j, 0.0).astype(f32), (8, 257))
    vis = np.broadcast_to((~forced).astype(f32), (8, 257))
    m33 = np.zeros((128, 33), f32)
    for jl in range(33):
        for off, wgt in ((-1, 1.0), (0, 2.0), (1, 2.0), (2, 2.0), (3, 1.0)):
            nl = 4 * jl + off
            if 0 <= nl < 128:
                m33[nl, jl] = wgt
    return {"d8": d8, "fv_s": np.ascontiguousarray(fv), "vis_s": np.ascontiguousarray(vis), "m33": m33}


def sample_core_inputs(inputs, c, nsb, ncores):
    f32 = np.float32
    L = inputs["cache_kv_cmp"].shape[0]
    npool = inputs["cache_kv_cmp"].shape[1]
    nchk = 2 if ncores > 1 else 1
    chp = npool // nchk
    shp = chp // ncores
    b0 = c * nsb

    def shard(a):
        a = np.asarray(a, f32).reshape(L, nchk, ncores, shp * 128, 256)
        return np.ascontiguousarray(a[:, :, c])
    out = {}
    out["xs"] = np.ascontiguousarray(np.asarray(inputs["x_sample"], f32)[b0:b0 + nsb].reshape(nsb * 8, D_MODEL))
    out["cmp_sh"] = shard(inputs["cache_kv_cmp"])
    out["slc_sh"] = shard(inputs["cache_kv_slc"])
    out["swin"] = np.ascontiguousarray(np.asarray(inputs["state_win_kv"], f32)[:, b0:b0 + nsb].reshape(L, nsb, 512, 256))
    for nm, key in (("h0re", "state_ssm_re"), ("h0im", "state_ssm_im")):
        a = np.asarray(inputs[key], f32)[:, b0:b0 + nsb].reshape(L, nsb, 4, 8, 4, 16)
        out[nm] = np.ascontiguousarray(a.transpose(0, 3, 5, 2, 4, 1).reshape(L, 128, 16 * nsb))
    out["pt"] = np.ascontiguousarray(np.asarray(inputs["page_table"], np.int32)[b0:b0 + nsb].reshape(1, nsb * 128))
    return out


_NC_CACHE = {}


def kernel(**inputs):
    f32 = np.float32
    x_prompt = np.ascontiguousarray(inputs["x_prompt"], dtype=f32)
    NS = BATCH // N_CORES
    key = "main"
    if key not in _NC_CACHE:
        _NC_CACHE[key] = build(NS, SEQ, DEPTH)
    nc = _NC_CACHE[key]
    shared = {
        "w_in": np.ascontiguousarray(inputs["w_in"], dtype=f32),
        "w_att_out": np.ascontiguousarray(inputs["w_att_out"], dtype=f32),
        "w_ssm_out": np.ascontiguousarray(inputs["w_ssm_out"], dtype=f32),
        "w_o": np.ascontiguousarray(inputs["w_o"], dtype=f32),
        "ln_g": np.ascontiguousarray(inputs["ln_g"], dtype=f32),
        "ln_b": np.ascontiguousarray(inputs["ln_b"], dtype=f32),
        "ident": np.eye(128, dtype=f32),
    }
    shared.update(s5_host_layouts(inputs))
    shared.update(cmp_host_layouts(inputs))
    shared.update(attn_host_consts(SEQ))
    shared.update(sample_host_consts())
    NSB = DEC_BATCH // N_CORES
    in_maps = []
    for c in range(N_CORES):
        m = dict(shared)
        m["x"] = x_prompt[c * NS:(c + 1) * NS]
        m.update(sample_core_inputs(inputs, c, NSB, N_CORES))
        in_maps.append(m)
    res = run_bass_kernel_spmd(nc, in_maps, core_ids=list(range(N_CORES)))
    R = res.results
    y_prompt = np.concatenate([R[c]["y"] for c in range(N_CORES)], axis=0)
    kvc = np.concatenate([R[c]["kvc"] for c in range(N_CORES)], axis=1).reshape(DEPTH, BATCH, SEQ, 2, 2, 64)
    kvs = np.concatenate([R[c]["kvs"] for c in range(N_CORES)], axis=1).reshape(DEPTH, BATCH, SEQ, 2, 2, 64)
    kvw = np.concatenate([R[c]["kvw"] for c in range(N_CORES)], axis=1).reshape(DEPTH, BATCH, WINDOW, 2, 2, 64)
    z = lambda *s: np.zeros(s, f32)
    sre = np.concatenate([R[c]["ssm_re"] for c in range(N_CORES)], axis=1)
    sim_ = np.concatenate([R[c]["ssm_im"] for c in range(N_CORES)], axis=1)
    cat = lambda k, ax: np.concatenate([R[c][k] for c in range(N_CORES)], axis=ax)
    y_sample = cat("ys", 0).reshape(DEC_BATCH, DEC_SEQ, D_MODEL)
    skvc = cat("skvc", 1).reshape(DEPTH, DEC_BATCH, DEC_SEQ, 2, 2, 64)
    skvs = cat("skvs", 1).reshape(DEPTH, DEC_BATCH, DEC_SEQ, 2, 2, 64)
    swin = cat("swin_o", 1).reshape(DEPTH, DEC_BATCH, WINDOW, 2, 2, 64)
    return (y_prompt, y_sample, kvc, kvs, kvw, sre, sim_, skvc, skvs, swin, cat("sssm_re", 1), cat("sssm_im", 1))
```
